# Optimizing a Trainium2 kernel written in Bass

```python
import jax, jax.numpy as jnp
from jax import lax
import numpy as np

D_MODEL = 1024
BATCH = 8
SEQ = 8192
DEPTH = 4

POOL_WINDOWS = (2, 4, 8, 16)
N_POOL_GROUPS = 4
POOL_GROUP_DIM = D_MODEL // 16
POOL_WIDTH = N_POOL_GROUPS * POOL_GROUP_DIM
HEAD_DIM = 64
N_Q_HEADS = D_MODEL // 128
N_KV_HEADS = 2
GROUP = N_Q_HEADS // N_KV_HEADS
Q_WIDTH = N_Q_HEADS * HEAD_DIM
KV_WIDTH = N_KV_HEADS * HEAD_DIM
WINDOW = 128
ROT_DIM = HEAD_DIM // 4
ROPE_THETA = 500000.0
CONV_WIDTH = D_MODEL // 4
CONV_K = 3
N_BRANCHES = 3
IN_WIDTH = POOL_WIDTH + Q_WIDTH + 2 * KV_WIDTH + 3 * CONV_WIDTH + N_BRANCHES * D_MODEL
D_FF = ((8 * D_MODEL // 3 + 127) // 128) * 128
FFN_K = 3
ALPHA = (2 * DEPTH) ** 0.25
BETA = (8 * DEPTH) ** -0.25
LN_EPS = 1e-5
MASK_VALUE = -1e30

kernel_name = "hybrid_pool_swa_shortconv_deepnorm"


def _split_points(sizes):
    pts, acc = [], 0
    for s in sizes[:-1]:
        acc += s
        pts.append(acc)
    return pts


def layer_norm(x, g, b):
    xf = x.astype(jnp.float32)
    mu = jnp.mean(xf, axis=-1, keepdims=True)
    var = jnp.mean(jnp.square(xf - mu), axis=-1, keepdims=True)
    y = (xf - mu) * lax.rsqrt(var + LN_EPS)
    return (y * g.astype(jnp.float32) + b.astype(jnp.float32)).astype(x.dtype)


def causal_dwconv(u, w):
    K = w.shape[0]
    S = u.shape[1]
    up = jnp.pad(u, ((0, 0), (K - 1, 0), (0, 0)))
    y = up[:, 0:S] * w[0]
    for k in range(1, K):
        y = y + up[:, k:k + S] * w[k]
    return y


def rope_tables(positions):
    inv_freq = ROPE_THETA ** (-jnp.arange(0, ROT_DIM, 2, dtype=jnp.float32) / ROT_DIM)
    ang = positions.astype(jnp.float32)[..., None] * inv_freq
    return jnp.cos(ang), jnp.sin(ang)


def apply_partial_rope(t, cos, sin):
    tf = t.astype(jnp.float32)
    half = ROT_DIM // 2
    x1, x2, rest = tf[..., :half], tf[..., half:ROT_DIM], tf[..., ROT_DIM:]
    c, s = cos[:, :, None, :], sin[:, :, None, :]
    out = jnp.concatenate([x1 * c - x2 * s, x2 * c + x1 * s, rest], axis=-1)
    return out.astype(t.dtype)


def multiscale_pool_mixer(u, w_pool, pool_scale):
    B, S, _ = u.shape
    ug = u.reshape(B, S, N_POOL_GROUPS, POOL_GROUP_DIM).astype(jnp.float32)
    csum = jnp.cumsum(ug, axis=1)
    t = jnp.arange(S)
    outs = []
    for g, w in enumerate(POOL_WINDOWS):
        c = csum[:, :, g]
        c_lag = jnp.pad(c[:, :S - w], ((0, 0), (w, 0), (0, 0)))
        count = jnp.minimum(t + 1, w).astype(jnp.float32)[None, :, None]
        outs.append((c - c_lag) / count - ug[:, :, g])
    pooled = jnp.stack(outs, axis=2).astype(u.dtype)
    mixed = jnp.einsum('bsgc,gcd->bsgd', pooled, w_pool)
    return mixed.reshape(B, S, POOL_WIDTH) * pool_scale


def sliding_window_gqa_sinks(q, k, v, sinks):
    B, S, _, D = q.shape
    nb = S // WINDOW
    qb = q.reshape(B, nb, WINDOW, N_KV_HEADS, GROUP, D)
    kb = k.reshape(B, nb, WINDOW, N_KV_HEADS, D)
    vb = v.reshape(B, nb, WINDOW, N_KV_HEADS, D)
    pad = ((0, 0), (1, 0), (0, 0), (0, 0), (0, 0))
    kcat = jnp.concatenate([jnp.pad(kb[:, :-1], pad), kb], axis=2)
    vcat = jnp.concatenate([jnp.pad(vb[:, :-1], pad), vb], axis=2)
    scores = jnp.einsum('bnqhgd,bnkhd->bnhgqk', qb, kcat).astype(jnp.float32)
    scores = scores * (HEAD_DIM ** -0.5)
    i = jnp.arange(WINDOW)[:, None]
    j = jnp.arange(2 * WINDOW)[None, :]
    band = (j > i) & (j <= i + WINDOW)
    blk = jnp.arange(nb)[:, None, None]
    valid = band[None] & ((blk > 0) | (j[None] >= WINDOW))
    scores = jnp.where(valid[None, :, None, None], scores, MASK_VALUE)
    sink = jnp.broadcast_to(
        sinks.astype(jnp.float32).reshape(1, 1, N_KV_HEADS, GROUP, 1, 1),
        scores.shape[:-1] + (1,))
    probs = jax.nn.softmax(jnp.concatenate([scores, sink], axis=-1), axis=-1)[..., :-1]
    out = jnp.einsum('bnhgqk,bnkhd->bnqhgd', probs.astype(v.dtype), vcat)
    return out.reshape(B, S, N_Q_HEADS * D)


def short_gated_conv(xc, gate_b, gate_c, conv_w):
    return gate_b * causal_dwconv(gate_c * xc, conv_w)


def hybrid_layer(x, cos, sin, w_in, w_pool, pool_scale, attn_sinks, conv_w,
                 w_branch_a, w_branch_b, w_branch_c, w_o, ln1_g, ln1_b,
                 w_up, ffn_conv_w, w_down, ln2_g, ln2_b):
    B, S, _ = x.shape
    proj = jnp.einsum('bsd,de->bse', x, w_in)
    sizes = [POOL_WIDTH, Q_WIDTH, KV_WIDTH, KV_WIDTH,
             CONV_WIDTH, CONV_WIDTH, CONV_WIDTH, N_BRANCHES * D_MODEL]
    u_pool, q, k, v, xc, gate_b, gate_c, gate_logits = jnp.split(
        proj, _split_points(sizes), axis=-1)
    o_a = multiscale_pool_mixer(u_pool, w_pool, pool_scale)
    q = apply_partial_rope(q.reshape(B, S, N_Q_HEADS, HEAD_DIM), cos, sin)
    k = apply_partial_rope(k.reshape(B, S, N_KV_HEADS, HEAD_DIM), cos, sin)
    v = v.reshape(B, S, N_KV_HEADS, HEAD_DIM)
    o_b = sliding_window_gqa_sinks(q, k, v, attn_sinks)
    o_c = short_gated_conv(xc, gate_b, gate_c, conv_w)
    gates = jax.nn.sigmoid(gate_logits).reshape(B, S, N_BRANCHES, D_MODEL)
    merged = (gates[:, :, 0] * jnp.einsum('bsc,cd->bsd', o_a, w_branch_a)
              + gates[:, :, 1] * jnp.einsum('bsc,cd->bsd', o_b, w_branch_b)
              + gates[:, :, 2] * jnp.einsum('bsc,cd->bsd', o_c, w_branch_c))
    mix = jnp.einsum('bsd,de->bse', merged, w_o)
    x = layer_norm(ALPHA * x + mix, ln1_g, ln1_b)
    up = causal_dwconv(jnp.einsum('bsd,df->bsf', x, w_up), ffn_conv_w)
    a, b = jnp.split(up, 2, axis=-1)
    ffn = jnp.einsum('bsf,fd->bsd', jax.nn.silu(a) * b, w_down)
    return layer_norm(ALPHA * x + ffn, ln2_g, ln2_b)


def setup_inputs(seed: int = 0) -> dict:
    key = jax.random.key(seed)
    ks = jax.random.split(key, 20)
    nrm = lambda k, shape, scale: jax.random.normal(k, shape, jnp.float32) * scale
    x = nrm(ks[0], (BATCH, SEQ, D_MODEL), 1.0)
    offset = jax.random.randint(ks[1], (BATCH, 1), 0, 1024, dtype=jnp.int32)
    positions = offset + jnp.arange(SEQ, dtype=jnp.int32)[None, :]
    return {
        "x": x,
        "positions": positions,
        "w_in": nrm(ks[2], (DEPTH, D_MODEL, IN_WIDTH), D_MODEL ** -0.5),
        "w_pool": nrm(ks[3], (DEPTH, N_POOL_GROUPS, POOL_GROUP_DIM, POOL_GROUP_DIM), POOL_GROUP_DIM ** -0.5),
        "pool_scale": 1.0 + nrm(ks[4], (DEPTH, POOL_WIDTH), 0.1),
        "attn_sinks": nrm(ks[5], (DEPTH, N_Q_HEADS), 0.5),
        "conv_w": nrm(ks[6], (DEPTH, CONV_K, CONV_WIDTH), CONV_K ** -0.5),
        "w_branch_a": nrm(ks[7], (DEPTH, POOL_WIDTH, D_MODEL), POOL_WIDTH ** -0.5),
        "w_branch_b": nrm(ks[8], (DEPTH, Q_WIDTH, D_MODEL), Q_WIDTH ** -0.5),
        "w_branch_c": nrm(ks[9], (DEPTH, CONV_WIDTH, D_MODEL), CONV_WIDTH ** -0.5),
        "w_o": nrm(ks[10], (DEPTH, D_MODEL, D_MODEL), BETA * D_MODEL ** -0.5),
        "ln1_g": 1.0 + nrm(ks[11], (DEPTH, D_MODEL), 0.02),
        "ln1_b": nrm(ks[12], (DEPTH, D_MODEL), 0.02),
        "w_up": nrm(ks[13], (DEPTH, D_MODEL, 2 * D_FF), D_MODEL ** -0.5),
        "ffn_conv_w": nrm(ks[14], (DEPTH, FFN_K, 2 * D_FF), FFN_K ** -0.5),
        "w_down": nrm(ks[15], (DEPTH, D_FF, D_MODEL), BETA * D_FF ** -0.5),
        "ln2_g": 1.0 + nrm(ks[16], (DEPTH, D_MODEL), 0.02),
        "ln2_b": nrm(ks[17], (DEPTH, D_MODEL), 0.02),
    }


def reference(x, positions, w_in, w_pool, pool_scale, attn_sinks, conv_w,
              w_branch_a, w_branch_b, w_branch_c, w_o, ln1_g, ln1_b,
              w_up, ffn_conv_w, w_down, ln2_g, ln2_b):
    cos, sin = rope_tables(positions)
    for l in range(DEPTH):
        x = hybrid_layer(x, cos, sin, w_in[l], w_pool[l], pool_scale[l], attn_sinks[l],
                         conv_w[l], w_branch_a[l], w_branch_b[l], w_branch_c[l], w_o[l],
                         ln1_g[l], ln1_b[l], w_up[l], ffn_conv_w[l], w_down[l],
                         ln2_g[l], ln2_b[l])
    return x
```

```python
import math
import os
import numpy as np
import concourse.bass as bass
import concourse.mybir as mybir
from concourse.bass_utils import run_bass_kernel_spmd

F32 = mybir.dt.float32
BF16 = mybir.dt.bfloat16
I32 = mybir.dt.int32
AF = mybir.ActivationFunctionType
ALU = mybir.AluOpType

D = 1024
IN_W = 4864
DFF = 2816
NTILE = 33
ALPHA = 8.0 ** 0.25
LN_EPS = 1e-5
VROWS_L = 172
PI = math.pi
DBG = int(os.environ.get('KDBG', '99'))
SKIP = os.environ.get('KSKIP', '').split(',')


class Sem:
    def __init__(self, name):
        self.name = name
        self.count = 0
        self.h = None


class Buf:
    __slots__ = ("name", "last_w", "readers", "gen", "aliases", "excl")

    def __init__(self, name, excl=False):
        self.name = name
        self.excl = excl
        self.last_w = None
        self.readers = {}
        self.gen = 0
        self.aliases = []


def _b(h):
    if isinstance(h, tuple):
        buf, gen = h
        assert buf.gen == gen, f"stale ring buffer {buf.name}: gen {gen} != {buf.gen}"
        return buf
    return h


class Rec:
    ENGS = ("pe", "act", "dve", "pool", "sp")

    def __init__(self):
        self.ops = {e: [] for e in self.ENGS}
        self.esem = {e: Sem("e_" + e) for e in self.ENGS}
        self.seen = {e: {} for e in self.ENGS}
        self.sems = list(self.esem.values())

    def new_sem(self, name):
        s = Sem(name)
        self.sems.append(s)
        return s

    def _collect(self, reads, writes, own):
        deps = {}

        def add(d):
            if d is None:
                return
            s, v = d
            if deps.get(s, 0) < v:
                deps[s] = v

        for h in reads:
            b = _b(h)
            add(b.last_w)
            if b.excl:
                for s, v in b.readers.items():
                    if s is not own:
                        add((s, v))
        for h in writes:
            b = _b(h)
            for bb in [b] + b.aliases:
                add(bb.last_w)
                for s, v in bb.readers.items():
                    add((s, v))
        return deps

    def _emit(self, e, fn, reads, writes, sem, step):
        deps = self._collect(reads, writes, sem)
        waits = []
        for s, v in deps.items():
            if e == "pe" and s is self.esem["pe"]:
                continue
            if self.seen[e].get(s, 0) >= v:
                continue
            self.seen[e][s] = v
            waits.append((s, v))
        sem.count += step
        me = (sem, sem.count)
        self.ops[e].append((waits, fn, sem, step))
        for h in reads:
            b = _b(h)
            if b.readers.get(sem, 0) < sem.count:
                b.readers[sem] = sem.count
        for h in writes:
            b = _b(h)
            b.last_w = me
            b.readers = {}
        return me

    def op(self, e, fn, reads=(), writes=()):
        return self._emit(e, fn, reads, writes, self.esem[e], 1)

    def dma(self, q, fn, dsem, reads=(), writes=()):
        return self._emit(q, fn, reads, writes, dsem, 16)

    def wait_only(self, e, dep):
        s, v = dep
        if self.seen[e].get(s, 0) >= v:
            return
        self.seen[e][s] = v
        self.ops[e].append(([(s, v)], None, None, 0))


class Ring:
    def __init__(self, name, n, excl=False):
        self.bufs = [Buf(f"{name}{i}", excl) for i in range(n)]
        self.n = n
        self.i = 0

    def alloc(self):
        k = self.i % self.n
        self.i += 1
        b = self.bufs[k]
        b.gen += 1
        return k, (b, b.gen)


def build(S, DEPTH, T, nslot=4, nf32=12, nbf=6):
    NSUB = T // 512
    NB = T // 128
    NCH = S // T
    assert S % T == 0 and T % 512 == 0

    nc = bass.Bass("TRN2", target_bir_lowering=False)
    R = Rec()

    def dram_in(name, shape, dt=F32):
        return nc.dram_tensor(name, list(shape), dt, kind="ExternalInput").ap()

    x_d = dram_in("x", [S, D])
    pos_d = dram_in("pos", [1, S], I32)
    w_in_d = dram_in("w_in", [DEPTH, D, IN_W])
    w_pool_d = dram_in("w_pool", [DEPTH, 4, 64, 64])
    wa_d = dram_in("w_branch_a", [DEPTH, 256, D])
    wb_d = dram_in("w_branch_b", [DEPTH, 512, D])
    wc_d = dram_in("w_branch_c", [DEPTH, 256, D])
    wo_d = dram_in("w_o", [DEPTH, D, D])
    wup_d = dram_in("w_up", [DEPTH, D, 2 * DFF])
    wdn_d = dram_in("w_down", [DEPTH, DFF, D])
    vecs_d = dram_in("vecs", [768, 128])
    sinks_d = dram_in("sinks", [1, 32])
    cf_d = dram_in("cf32", [128, 192])
    cb_d = dram_in("cbf", [128, 1408])
    out_d = nc.dram_tensor("out", [S, D], F32, kind="ExternalOutput").ap()
    wscr = nc.dram_tensor("wscr", [DEPTH * NTILE, 128, 4096], BF16).ap()

    def sb(name, shape, dt):
        return nc.alloc_sbuf_tensor(name, list(shape), dt)

    X = sb("X", [128, 8, T], F32)
    XB = sb("XB", [128, 8, T], BF16)
    HR = sb("HR", [128, 22, T], BF16)
    WS = [sb(f"WS{i}", [128, 4096], BF16) for i in range(nslot)]
    CF = sb("CF", [128, 192], F32)
    CB = sb("CB", [128, 1408], BF16)
    VR = sb("VR", [128, 6, 128], F32)
    VEC = sb("VEC", [128, 768], F32)
    WPOOL = sb("WPOOL", [128, DEPTH, 2, 128], BF16)
    SK = sb("SK", [128, 32], F32)
    EXPS = sb("EXPS", [128, 32], F32)
    CT = sb("CT", [128, T], F32)
    ST = sb("ST", [128, T], F32)
    POSI = sb("POSI", [128, T], I32)
    ANG = sb("ANG", [128, T], F32)
    TMPT = sb("TMPT", [128, T], F32)
    KT = [sb(f"KT{l}", [128, (NB + 1) * 128], BF16) for l in range(DEPTH)]
    VT = [sb(f"VT{l}", [128, NB + 1, 2, 65], BF16) for l in range(DEPTH)]
    UPS = [sb(f"UPS{l}", [128, 2, 15], F32) for l in range(DEPTH)]
    PPS = [sb(f"PPS{l}", [128, 2, 2], F32) for l in range(DEPTH)]
    UST = [sb(f"UST{l}", [128, 44, 2], F32) for l in range(DEPTH)]
    LNS = sb("LNS", [128, 8, 512], BF16)
    MEAN = sb("MEAN", [128, 512], F32)
    RSTD = sb("RSTD", [128, 512], F32)
    DEN = sb("DEN", [128, 2, 8], F32)
    RDEN = sb("RDEN", [128, 2, 8], F32)
    FR = [sb(f"FR{i}", [128, 528], F32) for i in range(nf32)]
    BR = [sb(f"BR{i}", [128, 1024], BF16) for i in range(nbf)]
    PLB = [sb(f"PLB{i}", [128, 2, 512], BF16) for i in range(NSUB)]
    PB = [sb(f"PB{i}", [128, 512], BF16) for i in range(10)]
    IO = [sb(f"IO{i}", [128, 1024], F32) for i in range(2)]
    XIN = [sb(f"XIN{i}", [128, 1024], F32) for i in range(NB)]
    PS = nc.alloc_psum_tensor("PS", [128, 8, 512], F32)

    def MGv(j, sl):
        return HR[:, j, sl]

    def OBv(cq, sl):
        return HR[:, 8 + cq, sl]

    def OAv(cc, sl):
        return HR[:, 12 + cc, sl]

    def OCv(h, sl):
        return HR[:, 14 + h, sl]

    def QTv(qc, sl):
        return HR[:, 16 + qc, sl]

    bX = [[Buf(f"X{j}_{s}") for s in range(NSUB)] for j in range(8)]
    bXB = [[Buf(f"XB{j}_{s}") for s in range(NSUB)] for j in range(8)]
    bH = [[Buf(f"H{f}_{s}") for s in range(NSUB)] for f in range(22)]
    bMG = [[Buf(f"MG{j}_{s}") for s in range(NSUB)] for j in range(8)]
    bOB = [[Buf(f"OB{j}_{s}") for s in range(NSUB)] for j in range(4)]
    bOA = [[Buf(f"OA{j}_{s}") for s in range(NSUB)] for j in range(2)]
    bOC = [[Buf(f"OC{j}_{s}") for s in range(NSUB)] for j in range(2)]
    bQT = [[Buf(f"QT{j}_{s}") for s in range(NSUB)] for j in range(4)]
    for s in range(NSUB):
        for j in range(8):
            bMG[j][s].aliases.append(bH[j][s]); bH[j][s].aliases.append(bMG[j][s])
        for j in range(4):
            bOB[j][s].aliases.append(bH[8 + j][s]); bH[8 + j][s].aliases.append(bOB[j][s])
            bQT[j][s].aliases.append(bH[16 + j][s]); bH[16 + j][s].aliases.append(bQT[j][s])
        for j in range(2):
            bOA[j][s].aliases.append(bH[12 + j][s]); bH[12 + j][s].aliases.append(bOA[j][s])
            bOC[j][s].aliases.append(bH[14 + j][s]); bH[14 + j][s].aliases.append(bOC[j][s])
    bWS = [Buf(f"WS{i}") for i in range(nslot)]
    sWS = [R.new_sem(f"ws{i}") for i in range(nslot)]
    bCONST = Buf("const")
    sCONST = R.new_sem("const")
    sCONSTP = R.new_sem("constp")
    bCB = Buf("CB")
    bVR = Buf("VR")
    bVEC = Buf("VEC")
    bWPOOL = Buf("WPOOL")
    bEXPS = Buf("EXPS")
    bSK = Buf("SK")
    bTAB = Buf("TAB")
    bPOSI = Buf("POSI")
    sPOS = R.new_sem("pos")
    bANG = Buf("ANG")
    bTMPT = Buf("TMPT")
    bKT = [[Buf(f"KT{l}_{s}") for s in range(NSUB + 1)] for l in range(DEPTH)]
    bVT = [[Buf(f"VT{l}_{s}") for s in range(NSUB + 1)] for l in range(DEPTH)]
    bUPS = [[Buf(f"UPS{l}_{c}") for c in range(2)] for l in range(DEPTH)]
    bPPS = [[Buf(f"PPS{l}_{c}") for c in range(2)] for l in range(DEPTH)]
    bUST = [[Buf(f"UST{l}_{f}") for f in range(44)] for l in range(DEPTH)]
    bLNS = [Buf(f"LNS{j}") for j in range(8)]
    bMEAN = Buf("MEAN")
    bRSTD = Buf("RSTD")
    bDEN = Buf("DEN")
    bDUM = Buf("DUM")
    bRDEN = Buf("RDEN")
    rFR = Ring("FR", nf32)
    rBR = Ring("BR", nbf)
    rPB = Ring("PB", 10)
    bPL = [Buf(f"PL{i}") for i in range(NSUB)]
    rPS = Ring("PS", 8, excl=True)
    rIO = Ring("IO", 2)
    sIO = [R.new_sem(f"io{i}") for i in range(2)]
    bXIN = [Buf(f"XIN{i}") for i in range(NB)]
    sXIN = [R.new_sem(f"xin{i}") for i in range(NB)]
    NGRP = 6
    GRP_OF_TILE = [0] * 4 + [1] * 4 + [2] * 4 + [3] * 6 + [4] * 7 + [5] * 8
    assert len(GRP_OF_TILE) == NTILE
    bWSCR = [[Buf(f"wscr{l}_{g}") for g in range(NGRP)] for l in range(DEPTH)]
    sWSCR = [[R.new_sem(f"wscr{l}_{g}") for g in range(NGRP)] for l in range(DEPTH)]

    def PSf(k):
        return PS[:, k, :]

    def PSb(k):
        return PS[:, k, :].bitcast(BF16)

    IDF = CF[:, 0:128]
    INVF = CF[:, 128:129]
    SIGN = CF[:, 129:130]
    INVC = CF[:, 130:162].rearrange("p (c t) -> p c t", c=2)
    IDB = CB[:, 0:128]
    RM = CB[:, 128:256]
    ONES = CB[:, 256:384]
    MCUR = CB[:, 384:896]
    MPREV = CB[:, 896:1408]

    def vcol(l, r):
        c = l * VROWS_L + r
        return VEC[:, c:c + 1]

    def I(method, **kw):
        return lambda e: getattr(e, method)(**kw)

    def IS(items):
        items = list(items)

        def fn(e):
            first = ins = None
            for m_, kw in items:
                ins = getattr(e, m_)(**kw)
                if first is None:
                    first = ins
            return first, ins
        return fn

    def act(fn, reads, writes):
        return R.op("act", fn, reads=reads, writes=writes)

    def dve(fn, reads, writes):
        return R.op("dve", fn, reads=reads, writes=writes)

    def pool(fn, reads, writes):
        return R.op("pool", fn, reads=reads, writes=writes)

    def pe(fn, reads, writes):
        return R.op("pe", fn, reads=reads, writes=writes)

    def mm_group(out_ap, pairs, reads, writes):
        n = len(pairs)
        return pe(IS([("matmul", dict(out=out_ap, lhsT=lt, rhs=rh, start=(i == 0), stop=(i == n - 1)))
                      for i, (lt, rh) in enumerate(pairs)]), reads, writes)

    def tsl(s):
        return slice(s * 512, (s + 1) * 512)

    for l in range(DEPTH):
        pool(I("memset", ap=UPS[l][:], constant=0.0), [], bUPS[l])
        pool(I("memset", ap=PPS[l][:], constant=0.0), [], bPPS[l])
        pool(I("memset", ap=UST[l][:], constant=0.0), [], bUST[l])
        pool(I("memset", ap=VT[l][:], constant=1.0), [], bVT[l])
        pool(I("memset", ap=KT[l][:, 0:128], constant=0.0), [], [bKT[l][0]])
    pool(I("memset", ap=WPOOL[:], constant=0.0), [], [bWPOOL])

    R.dma("sp", I("dma_start", out=CF[:], in_=cf_d[:, :]), sCONST, writes=[bCONST])
    if 'sk' not in SKIP:
        R.dma("sp", I("dma_start", out=SK[:], in_=sinks_d[0:1, :].partition_broadcast(128)), sCONST, writes=[bSK])
    if 'vr' not in SKIP:
        R.dma("sp", I("dma_start", out=VR[:], in_=vecs_d.rearrange("(g r) c -> r g c", r=128)), sCONST, writes=[bVR])
    if 'cb' not in SKIP:
        R.dma("pool", I("dma_start", out=CB[:], in_=cb_d[:, :]), sCONSTP, writes=[bCB])
    for l in range(DEPTH if 'wpool' not in SKIP else 0):
        for g in range(4):
            po = (g % 2) * 64
            R.dma("pool", I("dma_start", out=WPOOL[po:po + 64, l, g // 2, po:po + 64], in_=w_pool_d[l, g]),
                  sCONSTP, writes=[bWPOOL])
    for b in (bCONST, bSK, bVR):
        b.last_w = (sCONST, sCONST.count)
    for b in (bCB, bWPOOL):
        b.last_w = (sCONSTP, sCONSTP.count)

    def prep_list(l):
        groups = [[] for _ in range(NGRP)]

        def tA(t):
            return wscr[l * NTILE + t].rearrange("p (k c) -> p k c", k=8)

        def src(w2d, c0, ncol):
            return w2d[:, c0:c0 + ncol].rearrange("(k p) c -> p k c", p=128)

        def cp(t, dst, s_):
            groups[GRP_OF_TILE[t]].append((dst, s_))

        win = w_in_d[l]
        cp(0, tA(0)[:, :, 0:256], src(win, 0, 256))
        cp(0, tA(0)[:, :, 256:384], src(win, 1024, 128))
        cp(0, tA(0)[:, :, 384:512], src(win, 1536, 128))
        cp(1, tA(1)[:, :, 0:128], src(win, 1280, 128))
        cp(1, tA(1)[:, :, 128:256], src(win, 1152, 128))
        cp(1, tA(1)[:, :, 256:384], src(win, 1664, 128))
        cp(1, tA(1)[:, :, 384:512], src(win, 1408, 128))
        for qc in range(4):
            cp(2, tA(2)[:, :, qc * 128:qc * 128 + 64], src(win, 256 + 64 * qc, 64))
            cp(2, tA(2)[:, :, qc * 128 + 64:qc * 128 + 128], src(win, 256 + 64 * (4 + qc), 64))
        cp(3, tA(3)[:, :, 0:256], src(win, 768, 256))
        for j in range(8):
            t = tA(4 + j)
            for i in range(3):
                cp(4 + j, t[:, :, i * 128:(i + 1) * 128], src(win, 1792 + i * 1024 + j * 128, 128))
            cp(4 + j, t[:, 0:2, 384:512], src(wa_d[l], j * 128, 128))
            cp(4 + j, t[:, 2:6, 384:512], src(wb_d[l], j * 128, 128))
            cp(4 + j, t[:, 6:8, 384:512], src(wc_d[l], j * 128, 128))
        for h in range(2):
            cp(12 + h, tA(12 + h), src(wo_d[l], h * 512, 512))
        for m in range(11):
            cp(14 + m, tA(14 + m)[:, :, 0:256], src(wup_d[l], 256 * m, 256))
            cp(14 + m, tA(14 + m)[:, :, 256:512], src(wup_d[l], DFF + 256 * m, 256))
        for j in range(8):
            dst = wscr[l * NTILE + 25 + j][:, 0:2816].rearrange("p (k c) -> p k c", k=22)
            cp(25 + j, dst, wdn_d[l][:, j * 128:(j + 1) * 128].rearrange("(k p) c -> p k c", p=128))
        return groups

    prep_pending = []

    def prep_queue(l):
        for g, items in enumerate(prep_list(l)):
            for i, (d_, s_) in enumerate(items):
                prep_pending.append((l, g, d_, s_, i == len(items) - 1))

    def prep_issue(n):
        for _ in range(min(n, len(prep_pending))):
            l, g, d_, s_, last = prep_pending.pop(0)
            R.dma("pool", I("dma_start", out=d_, in_=s_), sWSCR[l][g])
            if last:
                bWSCR[l][g].last_w = (sWSCR[l][g], sWSCR[l][g].count)

    def prep_flush_layer(l):
        while any(p[0] <= l for p in prep_pending):
            prep_issue(1)

    if 'prep' not in SKIP:
        prep_queue(0)
        prep_flush_layer(0)

    for g in range(6 if 'vec' not in SKIP else 0):
        k, hp = rPS.alloc()
        pe(I("transpose", out=PSf(k)[:, 0:128], in_=VR[:, g, :], identity=IDF), [bVR, bCONST], [hp])
        act(I("activation", out=VEC[:, g * 128:(g + 1) * 128], in_=PSf(k)[:, 0:128], func=AF.Copy), [hp], [bVEC])
    if 'sk' not in SKIP:
        act(I("activation", out=EXPS[:], in_=SK[:], func=AF.Exp), [bSK], [bEXPS])

    NSEQ = NCH * DEPTH * NTILE
    wstate = {"next": 0}

    def load_ahead(upto):
        upto = min(upto, NSEQ - 1)
        while wstate["next"] <= upto:
            i = wstate["next"]
            wstate["next"] += 1
            l = (i // NTILE) % DEPTH
            t = i % NTILE
            sl = i % nslot
            bWS[sl].gen = i
            ncol = 2816 if t >= 25 else 4096
            if t == 3:
                o_ = WS[sl][:].rearrange("p (k c) -> p k c", k=8)[:, :, 0:256]
                i_ = wscr[l * NTILE + t].rearrange("p (k c) -> p k c", k=8)[:, :, 0:256]
            else:
                o_ = WS[sl][:, 0:ncol]
                i_ = wscr[l * NTILE + t][:, 0:ncol]
            assert bWSCR[l][GRP_OF_TILE[t]].last_w is not None or 'prep' in SKIP, (l, t)
            R.dma("sp", I("dma_start", out=o_, in_=i_), sWS[sl], reads=[bWSCR[l][GRP_OF_TILE[t]]], writes=[bWS[sl]])

    def wtile(c, l, t, hold=0):
        i = (c * DEPTH + l) * NTILE + t
        if c == 0:
            prep_issue(3)
        load_ahead(i + nslot - 1 - hold)
        sl = i % nslot
        return sl, (bWS[sl], i)

    def WA(sl, n, k):
        return WS[sl][:, k * 512 + n * 128:k * 512 + n * 128 + 128]

    def proj(sl, hw, n, s):
        k, hp = rPS.alloc()
        mm_group(PSf(k), [(WA(sl, n, kk), XB[:, kk, tsl(s)]) for kk in range(8)],
                 [hw] + [bXB[kk][s] for kk in range(8)], [hp])
        return k, hp

    def proj4(sl, hw, s):
        banks = [rPS.alloc() for _ in range(4)]
        for kk in range(8):
            pe(IS([("matmul", dict(out=PSf(banks[n][0]), lhsT=WA(sl, n, kk), rhs=XB[:, kk, tsl(s)],
                                   start=(kk == 0), stop=(kk == 7))) for n in range(4)]),
               [hw, bXB[kk][s]], [b_[1] for b_ in banks])
        return banks

    def ln_pre(j, s):
        sl_ = tsl(s)
        act(I("activation", out=XB[:, j, sl_], in_=X[:, j, sl_], func=AF.Copy), [bX[j][s]], [bXB[j][s]])
        act(I("activation", out=LNS[:, j, :], in_=X[:, j, sl_], func=AF.Square), [bX[j][s]], [bLNS[j]])
        if j == 7:
            act(I("activation", out=DEN[:, 1, 4:5], in_=CF[:, 0:1], func=AF.Ln, bias=1.0, scale=1.0), [bCONST], [bDUM])

    def ln_stats(s, js, st):
        sl_ = tsl(s)
        first = st is None
        if first:
            st = (rPS.alloc(), rPS.alloc())
        (km, hm), (kq, hq) = st
        last = (js[-1] == 7)
        pe(IS([("matmul", dict(out=PSf(km), lhsT=ONES, rhs=XB[:, j, sl_], start=(first and j == js[0]),
                               stop=(last and j == 7))) for j in js]), [bCB] + [bXB[j][s] for j in js], [hm])
        pe(IS([("matmul", dict(out=PSf(kq), lhsT=ONES, rhs=LNS[:, j, :], start=(first and j == js[0]),
                               stop=(last and j == 7))) for j in js]), [bCB] + [bLNS[j] for j in js], [hq])
        return st

    def ln_post(l, s, grow, brow, st=None):
        sl_ = tsl(s)
        if st is None:
            st = ln_stats(s, list(range(8)), None)
        else:
            st = ln_stats(s, [7], st)
        (km, hm), (kq, hq) = st
        act(I("activation", out=MEAN[:], in_=PSf(km), func=AF.Copy), [hm], [bMEAN])
        k2, h2 = rFR.alloc()
        V2 = FR[k2][:, 0:512]
        dve(I("tensor_tensor", out=V2, in0=MEAN[:], in1=MEAN[:], op=ALU.mult), [bMEAN], [h2])
        dve(I("tensor_tensor", out=V2, in0=PSf(kq), in1=V2, op=ALU.subtract), [hq, h2], [h2])
        act(I("activation", out=V2, in_=V2, func=AF.Ln, bias=LN_EPS, scale=1.0), [h2], [h2])
        act(I("activation", out=RSTD[:], in_=V2, func=AF.Exp, scale=-0.5), [h2], [bRSTD])
        dve(I("scalar_tensor_tensor", out=MEAN[:], in0=MEAN[:], scalar=-1.0, in1=RSTD[:], op0=ALU.mult, op1=ALU.mult),
            [bMEAN, bRSTD], [bMEAN])
        for j in range(8):
            kx, hx = rFR.alloc()
            XN = FR[kx][:, 0:512]
            dve(I("tensor_tensor", out=XN, in0=X[:, j, sl_], in1=RSTD[:], op=ALU.mult), [bX[j][s], bRSTD], [hx])
            dve(I("tensor_tensor", out=XN, in0=XN, in1=MEAN[:], op=ALU.add), [hx, bMEAN], [hx])
            act(I("activation", out=XB[:, j, sl_], in_=XN, func=AF.Identity, scale=vcol(l, grow + j),
                  bias=vcol(l, brow + j)), [hx, bVEC], [bXB[j][s]])
            act(I("activation", out=X[:, j, sl_], in_=XN, func=AF.Identity, scale=vcol(l, grow + j),
                  bias=vcol(l, brow + j)), [hx, bVEC], [bX[j][s]])

    def prefetch_x(c):
        tok0 = c * T
        for tb in range(NB):
            R.dma("sp", I("dma_start", out=XIN[tb][:], in_=x_d[tok0 + tb * 128:tok0 + (tb + 1) * 128, :]), sXIN[tb],
                  writes=[bXIN[tb]])

    def tables(c):
        tok0 = c * T
        R.dma("sp", I("dma_start", out=POSI[:], in_=pos_d[0:1, tok0:tok0 + T].partition_broadcast(128)), sPOS,
              writes=[bPOSI])
        dve(I("tensor_copy", out=ANG[:], in_=POSI[:]), [bPOSI], [bANG])
        dve(I("tensor_scalar", out=ANG[:], in0=ANG[:], scalar1=INVF, scalar2=None, op0=ALU.mult), [bANG, bCONST], [bANG])
        C1 = 6.28125
        C2 = 2 * PI - 6.28125
        for (TB_, off) in ((ST, 0.0), (CT, 0.5 * PI)):
            dve(I("tensor_scalar", out=TB_[:], in0=ANG[:], scalar1=off, scalar2=None, op0=ALU.add), [bANG], [bTAB])
            dve(I("tensor_scalar", out=POSI[:], in0=TB_[:], scalar1=1.0 / (2 * PI), scalar2=None, op0=ALU.mult),
                [bTAB], [bPOSI])
            dve(I("tensor_copy", out=TMPT[:], in_=POSI[:]), [bPOSI], [bTMPT])
            dve(I("scalar_tensor_tensor", out=TB_[:], in0=TMPT[:], scalar=-C1, in1=TB_[:], op0=ALU.mult, op1=ALU.add),
                [bTMPT, bTAB], [bTAB])
            dve(I("scalar_tensor_tensor", out=TB_[:], in0=TMPT[:], scalar=-C2, in1=TB_[:], op0=ALU.mult, op1=ALU.add),
                [bTMPT, bTAB], [bTAB])
            dve(I("tensor_scalar", out=TB_[:], in0=TB_[:], scalar1=-PI, scalar2=PI, op0=ALU.max, op1=ALU.min),
                [bTAB], [bTAB])
            act(I("activation", out=TB_[:], in_=TB_[:], func=AF.Sin), [bTAB], [bTAB])
        dve(I("tensor_scalar", out=ST[:], in0=ST[:], scalar1=SIGN, scalar2=None, op0=ALU.mult), [bTAB, bCONST], [bTAB])

    def prologue(c):
        for tb in range(NB if 'xin' not in SKIP else 0):
            s = tb // 4
            xs = slice(tb * 128, (tb + 1) * 128)
            for half in range(2):
                k, hp = rPS.alloc()
                pe(IS([("transpose", dict(out=PSf(k)[:, q * 128:(q + 1) * 128],
                                          in_=XIN[tb][:, (half * 4 + q) * 128:(half * 4 + q + 1) * 128], identity=IDF))
                       for q in range(4)]), [bXIN[tb], bCONST], [hp])
                p3 = PSf(k).rearrange("p (q t) -> p q t", q=4)
                act(I("activation", out=X[:, half * 4:half * 4 + 4, xs], in_=p3, func=AF.Copy), [hp],
                    [bX[half * 4 + q][s] for q in range(4)])
                dve(I("tensor_copy", out=XB[:, half * 4:half * 4 + 4, xs], in_=p3), [hp],
                    [bXB[half * 4 + q][s] for q in range(4)])

    def layer(c, l):
        sl0, hw0 = wtile(c, l, 0)
        t0banks = [proj4(sl0, hw0, s) for s in range(NSUB)]
        for s in range(NSUB):
            hpl = bPL[s]
            PL = PLB[s]
            first = (c == 0 and s == 0)
            for cc in range(2):
                kp, hp = t0banks[s][cc]
                ku, hu = rFR.alloc()
                U = FR[ku]
                act(I("activation", out=U[:, 1:16], in_=UPS[l][:, cc, :], func=AF.Copy), [bUPS[l][cc]], [hu])
                act(I("activation", out=U[:, 16:528], in_=PSf(kp), func=AF.Copy), [hp], [hu])
                act(I("activation", out=UPS[l][:, cc, :], in_=U[:, 513:528], func=AF.Copy), [hu], [bUPS[l][cc]])
                k1, h1 = rFR.alloc()
                A1 = FR[k1]
                pool(I("tensor_tensor", out=A1[:, 2:528], in0=U[:, 2:528], in1=U[:, 1:527], op=ALU.add), [hu], [h1])
                k2, h2 = rFR.alloc()
                A2 = FR[k2]
                pool(I("tensor_tensor", out=A2[:, 4:528], in0=A1[:, 4:528], in1=A1[:, 2:526], op=ALU.add), [h1], [h2])
                if cc == 0:
                    grp = [(0, A1, h1, 0.5), (64, A2, h2, 0.25)]
                else:
                    k3, h3 = rFR.alloc()
                    A3 = FR[k3]
                    pool(I("tensor_tensor", out=A3[:, 8:528], in0=A2[:, 8:528], in1=A2[:, 4:524], op=ALU.add),
                         [h2], [h3])
                    k4, h4 = rFR.alloc()
                    A4 = FR[k4]
                    pool(I("tensor_tensor", out=A4[:, 16:528], in0=A3[:, 16:528], in1=A3[:, 8:520], op=ALU.add),
                         [h3], [h4])
                    grp = [(0, A3, h3, 0.125), (64, A4, h4, 0.0625)]
                for (p0, A, hA, inv) in grp:
                    ps_ = slice(p0, p0 + 64)
                    dve(I("scalar_tensor_tensor", out=PL[ps_, cc, :], in0=A[ps_, 16:528], scalar=inv,
                          in1=U[ps_, 16:528], op0=ALU.mult, op1=ALU.subtract), [hA, hu], [hpl])
                    if first:
                        kf, hf = rFR.alloc()
                        pool(I("tensor_tensor", out=FR[kf][ps_, 0:15], in0=A[ps_, 16:31], in1=INVC[ps_, cc, 0:15],
                               op=ALU.mult), [hA, bCONST], [hf])
                        pool(I("tensor_tensor", out=PL[ps_, cc, 0:15], in0=FR[kf][ps_, 0:15], in1=U[ps_, 16:31],
                               op=ALU.subtract), [hf, hu], [hpl])

        if DBG <= 1:
            return
        sl1, hw1 = wtile(c, l, 1, hold=1)
        for s in range(NSUB):
            for h in range(2):
                (sx, hx_, nx) = (sl0, hw0, 2) if h == 0 else (sl1, hw1, 1)
                (sg, hg_, ng) = (sl0, hw0, 3) if h == 0 else (sl1, hw1, 2)
                nb_ = 0 if h == 0 else 3
                if h == 0:
                    (kxc, hxc), (kgc, hgc) = t0banks[s][2], t0banks[s][3]
                else:
                    kxc, hxc = proj(sx, hx_, nx, s)
                    kgc, hgc = proj(sg, hg_, ng, s)
                kgb, hgb = proj(sl1, hw1, nb_, s)
                kx, hx = rFR.alloc()
                XC = FR[kx][:, 0:512]
                act(I("activation", out=XC, in_=PSf(kxc), func=AF.Copy), [hxc], [hx])
                kpp, hpp = rFR.alloc()
                PP = FR[kpp]
                dve(I("tensor_copy", out=PP[:, 0:2], in_=PPS[l][:, h, :]), [bPPS[l][h]], [hpp])
                dve(I("tensor_tensor", out=PP[:, 2:514], in0=PSf(kgc), in1=XC, op=ALU.mult), [hgc, hx], [hpp])
                dve(I("tensor_copy", out=PPS[l][:, h, :], in_=PP[:, 512:514]), [hpp], [bPPS[l][h]])
                ky, hy = rFR.alloc()
                Y = FR[ky][:, 0:512]
                act(I("activation", out=Y, in_=PP[:, 0:512], func=AF.Identity, scale=vcol(l, 164 + h)),
                    [hpp, bVEC], [hy])
                dve(I("scalar_tensor_tensor", out=Y, in0=PP[:, 1:513], scalar=vcol(l, 166 + h), in1=Y,
                      op0=ALU.mult, op1=ALU.add), [hpp, hy, bVEC], [hy])
                dve(I("scalar_tensor_tensor", out=Y, in0=PP[:, 2:514], scalar=vcol(l, 168 + h), in1=Y,
                      op0=ALU.mult, op1=ALU.add), [hpp, hy, bVEC], [hy])
                dve(I("tensor_tensor", out=OCv(h, tsl(s)), in0=PSf(kgb), in1=Y, op=ALU.mult), [hgb, hy], [bOC[h][s]])

        if DBG <= 2:
            return
        sl2, hw2 = wtile(c, l, 2)
        sl3, hw3 = wtile(c, l, 3, hold=1)

        def rope_rot(qc, s, QB, hb_, T1, h1):
            kr, hr = rPS.alloc()
            mm_group(PSf(kr), [(RM, QB)], [bCB, hb_], [hr])
            k2, h2 = rFR.alloc()
            T2 = FR[k2][:, 0:512]
            dve(I("tensor_tensor", out=T2, in0=PSf(kr), in1=ST[:, tsl(s)], op=ALU.mult), [hr, bTAB], [h2])
            if qc < 4:
                dst = QTv(qc, tsl(s))
                wr = [bQT[qc][s]]
            else:
                dst = KT[l][:, 128 + s * 512:128 + (s + 1) * 512]
                wr = [bKT[l][s + 1]]
            pool(I("tensor_tensor", out=dst, in0=T1, in1=T2, op=ALU.add), [h1, h2], wr)

        for s in range(NSUB):
            pend = None
            for qc in (4, 0, 1, 2, 3):
                if qc < 4:
                    kq, hq = proj(sl2, hw2, qc, s)
                else:
                    kq, hq = proj(sl3, hw3, 0, s)
                kb_, hb_ = rBR.alloc()
                QB = BR[kb_][:, 0:512]
                act(I("activation", out=QB, in_=PSf(kq), func=AF.Copy), [hq], [hb_])
                k1, h1 = rFR.alloc()
                T1 = FR[k1][:, 0:512]
                dve(I("tensor_tensor", out=T1, in0=PSf(kq), in1=CT[:, tsl(s)], op=ALU.mult), [hq, bTAB], [h1])
                if pend is not None:
                    rope_rot(*pend)
                pend = (qc, s, QB, hb_, T1, h1)
            kv, hv = rPS.alloc()
            items = []
            for blk in range(4):
                ts_ = slice(s * 512 + blk * 128, s * 512 + (blk + 1) * 128)
                for kk in range(8):
                    items.append(("matmul", dict(out=PSf(kv)[:, blk * 128:(blk + 1) * 128], lhsT=XB[:, kk, ts_],
                                                 rhs=WA(sl3, 1, kk), start=(kk == 0), stop=(kk == 7))))
            pe(IS(items), [hw3] + [bXB[kk][s] for kk in range(8)], [hv])
            rope_rot(*pend)
            for blk in range(4):
                act(I("activation", out=VT[l][:, 1 + s * 4 + blk, :, 0:64],
                      in_=PSf(kv)[:, blk * 128:(blk + 1) * 128].rearrange("p (g d) -> p g d", g=2), func=AF.Copy),
                    [hv], [bVT[l][s + 1]])
        if DBG <= 3:
            return

        def attn_scores(n):
            s = n // 4
            kbs = [n - 1, n]
            if c == 0 and n == 0:
                kbs = [n]
            plist = []
            for g in range(2):
                gp = slice(g * 64, (g + 1) * 64)
                for kb in kbs:
                    sk = 0 if kb < 0 else 1 + kb // 4
                    kst, hst = rPS.alloc()
                    M = MCUR if kb == n else MPREV
                    pe(IS([("matmul", dict(out=PSf(kst), lhsT=KT[l][gp, (kb + 1) * 128:(kb + 2) * 128],
                                           rhs=HR[gp, 16:20, n * 128:(n + 1) * 128], start=True, stop=False)),
                           ("matmul", dict(out=PSf(kst), lhsT=IDB, rhs=M, start=False, stop=True))]),
                       [bKT[l][sk], bCB] + [bQT[q][s] for q in range(4)], [hst])
                    kp_, hp_ = rPB.alloc()
                    P = PB[kp_][:]
                    act(I("activation", out=P, in_=PSf(kst), func=AF.Exp, scale=0.125), [hst], [hp_])
                    plist.append((g, kb, sk, P, hp_))
            return plist, kbs

        def attn_pv(n, plist, kbs):
            s = n // 4
            ko = [rPS.alloc(), rPS.alloc()]
            for (g, kb, sk, P, hp_) in plist:
                pe(IS([("matmul", dict(out=PSf(ko[g][0])[:, hh * 65:(hh + 1) * 65],
                                       lhsT=P[:, hh * 128:(hh + 1) * 128], rhs=VT[l][:, kb + 1, g, :],
                                       start=(kb == kbs[0] and hh == 0), stop=(kb == kbs[-1] and hh == 3),
                                       skip_group_check=True)) for hh in range(4)]),
                   [hp_, bVT[l][sk]], [ko[g][1]])
            kob, hob = rBR.alloc()
            OBT = BR[kob][:, 0:512]
            for g in range(2):
                O3 = PSf(ko[g][0])[:, 0:260].rearrange("p (h d) -> p h d", h=4)
                dve(I("tensor_tensor", out=DEN[:, g, 0:4], in0=O3[:, :, 64],
                      in1=EXPS[:, l * 8 + g * 4:l * 8 + g * 4 + 4], op=ALU.add), [ko[g][1], bEXPS], [bDEN])
                dve(I("reciprocal", out=RDEN[:, g, 0:4], in_=DEN[:, g, 0:4]), [bDEN], [bRDEN])
                dve(I("tensor_tensor", out=OBT[:, g * 256:(g + 1) * 256].rearrange("p (h d) -> p h d", h=4),
                      in0=O3[:, :, 0:64], in1=RDEN[:, g, 0:4].unsqueeze(2).broadcast_to([128, 4, 64]), op=ALU.mult),
                    [ko[g][1], bRDEN], [hob])
            ktp, htp = rPS.alloc()
            pe(IS([("transpose", dict(out=PSb(ktp)[:, cq * 128:(cq + 1) * 128], in_=OBT[:, cq * 128:(cq + 1) * 128],
                                      identity=IDB)) for cq in range(4)]), [hob, bCB], [htp])
            dve(I("tensor_copy", out=HR[:, 8:12, n * 128:(n + 1) * 128],
                  in_=PSb(ktp)[:, 0:512].rearrange("p (q t) -> p q t", q=4)),
                [htp], [bOB[q][s] for q in range(4)])

        pend = None
        for n in range(NB):
            cur = attn_scores(n)
            if pend is not None:
                attn_pv(*pend)
            pend = (n,) + cur
        attn_pv(*pend)
        if l == DEPTH - 1 and c + 1 < NCH and 'tables' not in SKIP:
            tables(c + 1)
        pool(I("tensor_copy", out=KT[l][:, 0:128], in_=KT[l][:, NB * 128:(NB + 1) * 128]), [bKT[l][NSUB]], [bKT[l][0]])
        pool(I("tensor_copy", out=VT[l][:, 0, :, :], in_=VT[l][:, NB, :, :]), [bVT[l][NSUB]], [bVT[l][0]])

        if DBG <= 4:
            return
        for s in range(NSUB):
            for cc in range(2):
                km, hm = rPS.alloc()
                mm_group(PSf(km), [(WPOOL[:, l, cc, :], PLB[s][:, cc, :])], [bWPOOL, bPL[s]], [hm])
                act(I("activation", out=OAv(cc, tsl(s)), in_=PSf(km), func=AF.Identity, scale=vcol(l, 170 + cc)),
                    [hm, bVEC], [bOA[cc][s]])
        for j in range(8):
            slj, hwj = wtile(c, l, 4 + j)
            for s in range(NSUB):
                srcs = [(0, 2, [OAv(kk, tsl(s)) for kk in range(2)], [bOA[0][s], bOA[1][s]]),
                        (2, 6, [OBv(kk, tsl(s)) for kk in range(4)], [bOB[q][s] for q in range(4)]),
                        (6, 8, [OCv(kk, tsl(s)) for kk in range(2)], [bOC[0][s], bOC[1][s]])]
                ms = []
                for i in range(3):
                    kg, hg = proj(slj, hwj, i, s)
                    k0, k1_, aps, rb = srcs[i]
                    kbp, hbp = rPS.alloc()
                    mm_group(PSf(kbp), [(WA(slj, 3, kk), aps[kk - k0]) for kk in range(k0, k1_)], [hwj] + rb, [hbp])
                    ksg, hsg = rFR.alloc()
                    SG = FR[ksg][:, 0:512]
                    act(I("activation", out=SG, in_=PSf(kg), func=AF.Sigmoid), [hg], [hsg])
                    dve(I("tensor_tensor", out=SG, in0=PSf(kbp), in1=SG, op=ALU.mult), [hbp, hsg], [hsg])
                    ms.append((SG, hsg))
                pool(I("tensor_tensor", out=ms[0][0], in0=ms[0][0], in1=ms[1][0], op=ALU.add),
                     [ms[0][1], ms[1][1]], [ms[0][1]])
                pool(I("tensor_tensor", out=MGv(j, tsl(s)), in0=ms[0][0], in1=ms[2][0], op=ALU.add),
                     [ms[0][1], ms[2][1]], [bMG[j][s]])

        if DBG <= 5:
            return
        slo = [wtile(c, l, 12), wtile(c, l, 13, hold=1)]
        for s in range(NSUB):
            for j in range(8):
                so, ho = slo[j // 4]
                km, hm = rPS.alloc()
                mm_group(PSf(km), [(WA(so, j % 4, kk), MGv(kk, tsl(s))) for kk in range(8)],
                         [ho] + [bMG[kk][s] for kk in range(8)], [hm])
                if j == 7:
                    st = ln_stats(s, list(range(7)), None)
                dve(I("scalar_tensor_tensor", out=X[:, j, tsl(s)], in0=X[:, j, tsl(s)], scalar=ALPHA, in1=PSf(km),
                      op0=ALU.mult, op1=ALU.add), [hm, bX[j][s]], [bX[j][s]])
                ln_pre(j, s)
            ln_post(l, s, 132, 140, st)

        if DBG <= 6:
            return
        for m in range(11):
            slm, hwm = wtile(c, l, 14 + m)
            for s in range(NSUB):
                mb = proj4(slm, hwm, s) if (m == 0 and NSUB == 1) else None
                for i in range(2):
                    fa = 2 * m + i
                    fb = 22 + fa
                    ys = []
                    for (f, n_) in ((fa, i), (fb, 2 + i)):
                        kp, hp = mb[n_] if mb is not None else proj(slm, hwm, n_, s)
                        P = PSf(kp)
                        HL = UST[l][:, f, :]
                        ky, hy = rFR.alloc()
                        Y = FR[ky][:, 0:512]
                        w0, w1, w2 = vcol(l, f), vcol(l, 44 + f), vcol(l, 88 + f)
                        act(I("activation", out=Y[:, 2:512], in_=P[:, 0:510], func=AF.Identity, scale=w0),
                            [hp, bVEC], [hy])
                        act(I("activation", out=Y[:, 0:2], in_=HL, func=AF.Identity, scale=w0),
                            [bUST[l][f], bVEC], [hy])
                        dve(I("scalar_tensor_tensor", out=Y[:, 0:1], in0=HL[:, 1:2], scalar=w1, in1=Y[:, 0:1],
                              op0=ALU.mult, op1=ALU.add), [bUST[l][f], hy, bVEC], [hy])
                        dve(I("scalar_tensor_tensor", out=Y[:, 1:512], in0=P[:, 0:511], scalar=w1, in1=Y[:, 1:512],
                              op0=ALU.mult, op1=ALU.add), [hp, hy, bVEC], [hy])
                        dve(I("scalar_tensor_tensor", out=Y, in0=P, scalar=w2, in1=Y,
                              op0=ALU.mult, op1=ALU.add), [hp, hy, bVEC], [hy])
                        dve(I("tensor_copy", out=HL, in_=P[:, 510:512]), [hp], [bUST[l][f]])
                        ys.append((Y, hy))
                    (YA, hya), (YB, hyb) = ys
                    act(I("activation", out=YA, in_=YA, func=AF.Silu), [hya], [hya])
                    pool(I("tensor_tensor", out=HR[:, fa, tsl(s)], in0=YA, in1=YB, op=ALU.mult), [hya, hyb], [bH[fa][s]])

        if DBG <= 7:
            return
        for j in range(8):
            sld, hwd = wtile(c, l, 25 + j)
            for s in range(NSUB):
                kd, hd = rPS.alloc()
                if j == 0:
                    for kk in range(22):
                        pe(I("matmul", out=PSf(kd), lhsT=WS[sld][:, kk * 128:(kk + 1) * 128], rhs=HR[:, kk, tsl(s)],
                             start=(kk == 0), stop=(kk == 21)), [hwd, bH[kk][s]], [hd])
                else:
                    mm_group(PSf(kd), [(WS[sld][:, kk * 128:(kk + 1) * 128], HR[:, kk, tsl(s)]) for kk in range(22)],
                             [hwd] + [bH[kk][s] for kk in range(22)], [hd])
                st2 = None
                if NSUB == 1 and j == 7:
                    st2 = ln_stats(s, list(range(7)), None)
                dve(I("scalar_tensor_tensor", out=X[:, j, tsl(s)], in0=X[:, j, tsl(s)], scalar=ALPHA, in1=PSf(kd),
                      op0=ALU.mult, op1=ALU.add), [hd, bX[j][s]], [bX[j][s]])
                if NSUB == 1:
                    ln_pre(j, s)
        for s in range(NSUB):
            if NSUB > 1:
                for j in range(8):
                    ln_pre(j, s)
            ln_post(l, s, 148, 156, st2 if NSUB == 1 else None)

    def epilogue(c):
        tok0 = c * T
        for tb in range(NB):
            s = tb // 4
            ki, hi = rIO.alloc()
            for half in range(2 if 'epi' not in SKIP else 0):
                k, hp = rPS.alloc()
                pe(IS([("transpose", dict(out=PSf(k)[:, q * 128:(q + 1) * 128],
                                          in_=X[:, half * 4 + q, tb * 128:(tb + 1) * 128], identity=IDF))
                       for q in range(4)]), [bX[half * 4 + q][s] for q in range(4)] + [bCONST], [hp])
                if half == 0:
                    act(I("activation", out=IO[ki][:, 0:512], in_=PSf(k), func=AF.Copy), [hp], [hi])
                else:
                    dve(I("tensor_copy", out=IO[ki][:, 512:1024], in_=PSf(k)), [hp], [hi])
            R.dma("sp", I("dma_start", out=out_d[tok0 + tb * 128:tok0 + (tb + 1) * 128, :], in_=IO[ki][:]), sIO[ki],
                  reads=[hi])

    prefetch_x(0)
    if 'tables' not in SKIP:
        tables(0)
    for c in range(NCH):
        prologue(c)
        for l in range(DEPTH):
            if l == DEPTH - 1 and c + 1 < NCH:
                prefetch_x(c + 1)
            if c == 0 and l + 1 < DEPTH and 'prep' not in SKIP:
                prep_queue(l + 1)
            if DBG > 0:
                layer(c, l)
            if c == 0 and l + 1 < DEPTH and 'prep' not in SKIP:
                prep_flush_layer(l + 1)
        epilogue(c)

    for i in range(2):
        R.wait_only("sp", (sIO[i], sIO[i].count))

    for s_ in R.sems:
        s_.h = nc.alloc_semaphore(s_.name)

    def run(eng, lst):
        for waits, fn, sem, step in lst:
            if fn is None:
                for (ws, wv) in waits:
                    eng.wait_ge(ws.h, wv)
                continue
            for (ws, wv) in waits[:-1]:
                eng.wait_ge(ws.h, wv)
            r = fn(eng)
            first, last = r if isinstance(r, tuple) else (r, r)
            if waits:
                first._wait_ge(waits[-1][0].h, waits[-1][1])
            last.then_inc(sem.h, step)

    with nc.Block() as block:
        @block.tensor
        def _(e):
            run(e, R.ops["pe"])

        @block.scalar
        def _(e):
            run(e, R.ops["act"])

        @block.vector
        def _(e):
            run(e, R.ops["dve"])

        @block.gpsimd
        def _(e):
            run(e, R.ops["pool"])

        @block.sync
        def _(e):
            run(e, R.ops["sp"])
    stats = {e: len(v) for e, v in R.ops.items()}
    stats["sem_counts"] = {s_.name: s_.count for s_ in R.sems}
    return nc, stats


def make_consts():
    cf = np.zeros((128, 192), np.float32)
    cf[:, 0:128] = np.eye(128, dtype=np.float32)
    inv_freq = (np.float32(500000.0) ** (-np.arange(0, 16, 2, dtype=np.float32) / np.float32(16))).astype(np.float32)
    for p in range(128):
        r = p % 64
        if r < 16:
            cf[p, 128] = inv_freq[r % 8]
            cf[p, 129] = -1.0 if r < 8 else 1.0
        for cc in range(2):
            w = (2, 4, 8, 16)[cc * 2 + (p // 64)]
            for t in range(16):
                cf[p, 130 + cc * 16 + t] = 1.0 / min(t + 1, w)
    cb = np.zeros((128, 1408), np.float32)
    cb[:, 0:128] = np.eye(128, dtype=np.float32)
    for m in range(128):
        r = m % 64
        if r < 8:
            cb[m + 8, 128 + m] = 1.0
        elif r < 16:
            cb[m - 8, 128 + m] = 1.0
    cb[:, 256:384] = 1.0 / 1024.0
    jj = np.arange(128)[:, None]
    ii = np.arange(128)[None, :]
    cur = np.where(jj <= ii, 0.0, -30000.0).astype(np.float32)
    prev = np.where(jj > ii, 0.0, -30000.0).astype(np.float32)
    cb[:, 384:896] = np.tile(cur, (1, 4))
    cb[:, 896:1408] = np.tile(prev, (1, 4))
    return cf, cb


def make_vecs(depth, ffn_conv_w, ln1_g, ln1_b, ln2_g, ln2_b, conv_w, pool_scale):
    rows = []
    for l in range(depth):
        rows.append(np.asarray(ffn_conv_w[l], np.float32).reshape(132, 128))
        rows.append(np.asarray(ln1_g[l], np.float32).reshape(8, 128))
        rows.append(np.asarray(ln1_b[l], np.float32).reshape(8, 128))
        rows.append(np.asarray(ln2_g[l], np.float32).reshape(8, 128))
        rows.append(np.asarray(ln2_b[l], np.float32).reshape(8, 128))
        rows.append(np.asarray(conv_w[l], np.float32).reshape(6, 128))
        rows.append(np.asarray(pool_scale[l], np.float32).reshape(2, 128))
    v = np.concatenate(rows, axis=0)
    out = np.zeros((768, 128), np.float32)
    out[:v.shape[0]] = v
    return out


_CACHE = {}


def run_model(inputs, S, DEPTH, T, n_cores, trace=False):
    key = (S, DEPTH, T)
    if key not in _CACHE:
        _CACHE[key] = build(S, DEPTH, T)
    nc, stats = _CACHE[key]
    cf, cb = make_consts()
    f = lambda k: np.ascontiguousarray(np.asarray(inputs[k], np.float32)[:DEPTH])
    vecs = make_vecs(DEPTH, inputs["ffn_conv_w"], inputs["ln1_g"], inputs["ln1_b"], inputs["ln2_g"], inputs["ln2_b"],
                     inputs["conv_w"], inputs["pool_scale"])
    sinks = np.zeros((1, 32), np.float32)
    sinks[0, :DEPTH * 8] = np.asarray(inputs["attn_sinks"], np.float32)[:DEPTH].reshape(-1)
    shared = {
        "w_in": f("w_in"), "w_pool": f("w_pool"), "w_branch_a": f("w_branch_a"), "w_branch_b": f("w_branch_b"),
        "w_branch_c": f("w_branch_c"), "w_o": f("w_o"), "w_up": f("w_up"), "w_down": f("w_down"),
        "vecs": vecs, "sinks": sinks, "cf32": cf, "cbf": cb,
    }
    x = np.asarray(inputs["x"], np.float32)
    pos = np.asarray(inputs["positions"], np.int32)
    in_maps = []
    for b in range(n_cores):
        m = dict(shared)
        m["x"] = np.ascontiguousarray(x[b, :S])
        m["pos"] = np.ascontiguousarray(pos[b, :S]).reshape(1, S)
        in_maps.append(m)
    res = run_bass_kernel_spmd(nc, in_maps, core_ids=list(range(n_cores)), trace=trace)
    out = np.stack([np.asarray(r["out"], np.float32) for r in res.results], axis=0)
    return out, res


def kernel(**inputs):
    out, _ = run_model(inputs, 8192, 4, 512, 8)
    return out
```

```python
import math
import os
import numpy as np
import concourse.bass as bass
import concourse.mybir as mybir
from concourse.bass_utils import run_bass_kernel_spmd

F32 = mybir.dt.float32
BF16 = mybir.dt.bfloat16
I32 = mybir.dt.int32
AF = mybir.ActivationFunctionType
ALU = mybir.AluOpType

D = 1024
IN_W = 4864
DFF = 2816
NTILE = 33
ALPHA = 8.0 ** 0.25
LN_EPS = 1e-5
VROWS_L = 172
PI = math.pi
DBG = int(os.environ.get('KDBG', '99'))
SKIP = os.environ.get('KSKIP', '').split(',')


class Sem:
    def __init__(self, name):
        self.name = name
        self.count = 0
        self.h = None


class Buf:
    __slots__ = ("name", "last_w", "readers", "gen", "aliases", "excl")

    def __init__(self, name, excl=False):
        self.name = name
        self.excl = excl
        self.last_w = None
        self.readers = {}
        self.gen = 0
        self.aliases = []


def _b(h):
    if isinstance(h, tuple):
        buf, gen = h
        assert buf.gen == gen, f"stale ring buffer {buf.name}: gen {gen} != {buf.gen}"
        return buf
    return h


class Rec:
    ENGS = ("pe", "act", "dve", "pool", "sp")

    def __init__(self):
        self.ops = {e: [] for e in self.ENGS}
        self.esem = {e: Sem("e_" + e) for e in self.ENGS}
        self.know = {e: {} for e in self.ENGS}
        self.sems = list(self.esem.values())
        self.hist = {s: {} for s in self.sems}

    def new_sem(self, name):
        s = Sem(name)
        self.sems.append(s)
        self.hist[s] = {}
        return s

    def _collect(self, reads, writes, own):
        deps = {}

        def add(d):
            if d is None:
                return
            s, v = d
            if deps.get(s, 0) < v:
                deps[s] = v

        for h in reads:
            b = _b(h)
            add(b.last_w)
            if b.excl:
                for s, v in b.readers.items():
                    if s is not own:
                        add((s, v))
        for h in writes:
            b = _b(h)
            for bb in [b] + b.aliases:
                add(bb.last_w)
                for s, v in bb.readers.items():
                    add((s, v))
        return deps

    def _emit(self, e, fn, reads, writes, sem, step):
        deps = self._collect(reads, writes, sem)
        know = self.know[e]
        cand = []
        for s, v in deps.items():
            if e == "pe" and s is self.esem["pe"]:
                continue
            if know.get(s, 0) >= v:
                continue
            cand.append((s, v))
        waits = []
        for i, (s, v) in enumerate(cand):
            covered = False
            for j, (s2, v2) in enumerate(cand):
                if j == i:
                    continue
                k2 = self.hist[s2].get(v2)
                if k2 is not None and k2.get(s, 0) >= v:
                    k1 = self.hist[s].get(v)
                    if k1 is not None and k1.get(s2, 0) >= v2 and i < j:
                        continue
                    covered = True
                    break
            if not covered:
                waits.append((s, v))
        for (s, v) in waits:
            ks = self.hist[s].get(v)
            if ks is not None:
                for a, b in ks.items():
                    if know.get(a, 0) < b:
                        know[a] = b
            if know.get(s, 0) < v:
                know[s] = v
        sem.count += step
        me = (sem, sem.count)
        snap = dict(know)
        snap[sem] = sem.count
        self.hist[sem][sem.count] = snap
        self.ops[e].append((waits, fn, sem, step))
        for h in reads:
            b = _b(h)
            if b.readers.get(sem, 0) < sem.count:
                b.readers[sem] = sem.count
        for h in writes:
            b = _b(h)
            b.last_w = me
            b.readers = {}
        return me

    def op(self, e, fn, reads=(), writes=()):
        return self._emit(e, fn, reads, writes, self.esem[e], 1)

    def dma(self, q, fn, dsem, reads=(), writes=()):
        return self._emit(q, fn, reads, writes, dsem, 16)

    def wait_only(self, e, dep):
        s, v = dep
        if self.know[e].get(s, 0) >= v:
            return
        self.know[e][s] = v
        self.ops[e].append(([(s, v)], None, None, 0))


class Ring:
    def __init__(self, name, n, excl=False):
        self.bufs = [Buf(f"{name}{i}", excl) for i in range(n)]
        self.n = n
        self.i = 0

    def alloc(self):
        k = self.i % self.n
        self.i += 1
        b = self.bufs[k]
        b.gen += 1
        return k, (b, b.gen)


def build(S, DEPTH, T, nslot=4, nf32=12, nbf=6):
    NSUB = T // 512
    NB = T // 128
    NCH = S // T
    assert S % T == 0 and T % 512 == 0

    nc = bass.Bass("TRN2", target_bir_lowering=False)
    R = Rec()

    def dram_in(name, shape, dt=F32):
        return nc.dram_tensor(name, list(shape), dt, kind="ExternalInput").ap()

    x_d = dram_in("x", [S, D])
    pos_d = dram_in("pos", [1, S], I32)
    w_in_d = dram_in("w_in", [DEPTH, D, IN_W])
    w_pool_d = dram_in("w_pool", [DEPTH, 4, 64, 64])
    wa_d = dram_in("w_branch_a", [DEPTH, 256, D])
    wb_d = dram_in("w_branch_b", [DEPTH, 512, D])
    wc_d = dram_in("w_branch_c", [DEPTH, 256, D])
    wo_d = dram_in("w_o", [DEPTH, D, D])
    wup_d = dram_in("w_up", [DEPTH, D, 2 * DFF])
    wdn_d = dram_in("w_down", [DEPTH, DFF, D])
    vecs_d = dram_in("vecs", [768, 128])
    sinks_d = dram_in("sinks", [1, 32])
    cf_d = dram_in("cf32", [128, 192])
    cb_d = dram_in("cbf", [128, 1408])
    out_d = nc.dram_tensor("out", [S, D], F32, kind="ExternalOutput").ap()
    wscr = nc.dram_tensor("wscr", [DEPTH * NTILE, 128, 4096], BF16).ap()

    def sb(name, shape, dt):
        return nc.alloc_sbuf_tensor(name, list(shape), dt)

    X = sb("X", [128, 8, T], F32)
    XB = sb("XB", [128, 8, T], BF16)
    HR = sb("HR", [128, 22, T], BF16)
    WS = [sb(f"WS{i}", [128, 4096], BF16) for i in range(nslot)]
    CF = sb("CF", [128, 192], F32)
    CB = sb("CB", [128, 1408], BF16)
    VR = sb("VR", [128, 6, 128], F32)
    VEC = sb("VEC", [128, 768], F32)
    WPOOL = sb("WPOOL", [128, DEPTH, 2, 128], BF16)
    SK = sb("SK", [128, 32], F32)
    EXPS = sb("EXPS", [128, 32], F32)
    CT = sb("CT", [128, T], F32)
    ST = sb("ST", [128, T], F32)
    POSI = sb("POSI", [128, T], I32)
    ANG = sb("ANG", [128, T], F32)
    TMPT = sb("TMPT", [128, T], F32)
    KT = [sb(f"KT{l}", [128, (NB + 1) * 128], BF16) for l in range(DEPTH)]
    VT = [sb(f"VT{l}", [128, NB + 1, 2, 65], BF16) for l in range(DEPTH)]
    UPS = [sb(f"UPS{l}", [128, 2, 15], F32) for l in range(DEPTH)]
    PPS = [sb(f"PPS{l}", [128, 2, 2], F32) for l in range(DEPTH)]
    UST = [sb(f"UST{l}", [128, 44, 2], F32) for l in range(DEPTH)]
    LNS = sb("LNS", [128, 8, 512], BF16)
    MEAN = sb("MEAN", [128, 512], F32)
    RSTD = sb("RSTD", [128, 512], F32)
    DEN = sb("DEN", [128, 2, 8], F32)
    RDEN = sb("RDEN", [128, 2, 8], F32)
    FR = [sb(f"FR{i}", [128, 528], F32) for i in range(nf32)]
    BR = [sb(f"BR{i}", [128, 1024], BF16) for i in range(nbf)]
    PLB = [sb(f"PLB{i}", [128, 2, 512], BF16) for i in range(NSUB)]
    PB = [sb(f"PB{i}", [128, 512], BF16) for i in range(10)]
    IO = [sb(f"IO{i}", [128, 1024], F32) for i in range(2)]
    XIN = [sb(f"XIN{i}", [128, 1024], F32) for i in range(NB)]
    PS = nc.alloc_psum_tensor("PS", [128, 8, 512], F32)

    def MGv(j, sl):
        return HR[:, j, sl]

    def OBv(cq, sl):
        return HR[:, 8 + cq, sl]

    def OAv(cc, sl):
        return HR[:, 12 + cc, sl]

    def OCv(h, sl):
        return HR[:, 14 + h, sl]

    def QTv(qc, sl):
        return HR[:, 16 + qc, sl]

    bX = [[Buf(f"X{j}_{s}") for s in range(NSUB)] for j in range(8)]
    bXB = [[Buf(f"XB{j}_{s}") for s in range(NSUB)] for j in range(8)]
    bH = [[Buf(f"H{f}_{s}") for s in range(NSUB)] for f in range(22)]
    bMG = [[Buf(f"MG{j}_{s}") for s in range(NSUB)] for j in range(8)]
    bOB = [[Buf(f"OB{j}_{s}") for s in range(NSUB)] for j in range(4)]
    bOA = [[Buf(f"OA{j}_{s}") for s in range(NSUB)] for j in range(2)]
    bOC = [[Buf(f"OC{j}_{s}") for s in range(NSUB)] for j in range(2)]
    bQT = [[Buf(f"QT{j}_{s}") for s in range(NSUB)] for j in range(4)]
    for s in range(NSUB):
        for j in range(8):
            bMG[j][s].aliases.append(bH[j][s]); bH[j][s].aliases.append(bMG[j][s])
        for j in range(4):
            bOB[j][s].aliases.append(bH[8 + j][s]); bH[8 + j][s].aliases.append(bOB[j][s])
            bQT[j][s].aliases.append(bH[16 + j][s]); bH[16 + j][s].aliases.append(bQT[j][s])
        for j in range(2):
            bOA[j][s].aliases.append(bH[12 + j][s]); bH[12 + j][s].aliases.append(bOA[j][s])
            bOC[j][s].aliases.append(bH[14 + j][s]); bH[14 + j][s].aliases.append(bOC[j][s])
    bWS = [Buf(f"WS{i}") for i in range(nslot)]
    sWS = [R.new_sem(f"ws{i}") for i in range(nslot)]
    bCONST = Buf("const")
    sCONST = R.new_sem("const")
    sCONSTP = R.new_sem("constp")
    bCB = Buf("CB")
    bVR = Buf("VR")
    bVEC = Buf("VEC")
    bWPOOL = Buf("WPOOL")
    bEXPS = Buf("EXPS")
    bSK = Buf("SK")
    bTAB = Buf("TAB")
    bPOSI = Buf("POSI")
    sPOS = R.new_sem("pos")
    bANG = Buf("ANG")
    bTMPT = Buf("TMPT")
    bKT = [[Buf(f"KT{l}_{s}") for s in range(NSUB + 1)] for l in range(DEPTH)]
    bVT = [[Buf(f"VT{l}_{s}") for s in range(NSUB + 1)] for l in range(DEPTH)]
    bUPS = [[Buf(f"UPS{l}_{c}") for c in range(2)] for l in range(DEPTH)]
    bPPS = [[Buf(f"PPS{l}_{c}") for c in range(2)] for l in range(DEPTH)]
    bUST = [[Buf(f"UST{l}_{f}") for f in range(44)] for l in range(DEPTH)]
    bLNS = [Buf(f"LNS{j}") for j in range(8)]
    bMEAN = Buf("MEAN")
    bRSTD = Buf("RSTD")
    bDEN = Buf("DEN")
    bDUM = Buf("DUM")
    bRDEN = Buf("RDEN")
    rFR = Ring("FR", nf32)
    rBR = Ring("BR", nbf)
    rPB = Ring("PB", 10)
    bPL = [Buf(f"PL{i}") for i in range(NSUB)]
    rPS = Ring("PS", 8, excl=True)
    rIO = Ring("IO", 2)
    sIO = [R.new_sem(f"io{i}") for i in range(2)]
    bXIN = [Buf(f"XIN{i}") for i in range(NB)]
    sXIN = [R.new_sem(f"xin{i}") for i in range(NB)]
    NGRP = 6
    GRP_OF_TILE = [0] * 4 + [1] * 4 + [2] * 4 + [3] * 6 + [4] * 7 + [5] * 8
    assert len(GRP_OF_TILE) == NTILE
    bWSCR = [[Buf(f"wscr{l}_{g}") for g in range(NGRP)] for l in range(DEPTH)]
    sWSCR = [[R.new_sem(f"wscr{l}_{g}") for g in range(NGRP)] for l in range(DEPTH)]

    def PSf(k):
        return PS[:, k, :]

    def PSb(k):
        return PS[:, k, :].bitcast(BF16)

    IDF = CF[:, 0:128]
    INVF = CF[:, 128:129]
    SIGN = CF[:, 129:130]
    INVC = CF[:, 130:162].rearrange("p (c t) -> p c t", c=2)
    IDB = CB[:, 0:128]
    RM = CB[:, 128:256]
    ONES = CB[:, 256:384]
    MCUR = CB[:, 384:896]
    MPREV = CB[:, 896:1408]

    def vcol(l, r):
        c = l * VROWS_L + r
        return VEC[:, c:c + 1]

    def I(method, **kw):
        return lambda e: getattr(e, method)(**kw)

    def IS(items):
        items = list(items)

        def fn(e):
            first = ins = None
            for m_, kw in items:
                ins = getattr(e, m_)(**kw)
                if first is None:
                    first = ins
            return first, ins
        return fn

    def act(fn, reads, writes):
        return R.op("act", fn, reads=reads, writes=writes)

    def dve(fn, reads, writes):
        return R.op("dve", fn, reads=reads, writes=writes)

    def pool(fn, reads, writes):
        return R.op("pool", fn, reads=reads, writes=writes)

    def pe(fn, reads, writes):
        return R.op("pe", fn, reads=reads, writes=writes)

    def mm_group(out_ap, pairs, reads, writes):
        n = len(pairs)
        return pe(IS([("matmul", dict(out=out_ap, lhsT=lt, rhs=rh, start=(i == 0), stop=(i == n - 1)))
                      for i, (lt, rh) in enumerate(pairs)]), reads, writes)

    def tsl(s):
        return slice(s * 512, (s + 1) * 512)

    for l in range(DEPTH):
        pool(I("memset", ap=UPS[l][:], constant=0.0), [], bUPS[l])
        pool(I("memset", ap=PPS[l][:], constant=0.0), [], bPPS[l])
        pool(I("memset", ap=UST[l][:], constant=0.0), [], bUST[l])
        pool(I("memset", ap=VT[l][:], constant=1.0), [], bVT[l])
        pool(I("memset", ap=KT[l][:, 0:128], constant=0.0), [], [bKT[l][0]])
    pool(I("memset", ap=WPOOL[:], constant=0.0), [], [bWPOOL])

    R.dma("sp", I("dma_start", out=CF[:], in_=cf_d[:, :]), sCONST, writes=[bCONST])
    if 'sk' not in SKIP:
        R.dma("sp", I("dma_start", out=SK[:], in_=sinks_d[0:1, :].partition_broadcast(128)), sCONST, writes=[bSK])
    if 'vr' not in SKIP:
        R.dma("sp", I("dma_start", out=VR[:], in_=vecs_d.rearrange("(g r) c -> r g c", r=128)), sCONST, writes=[bVR])
    if 'cb' not in SKIP:
        R.dma("pool", I("dma_start", out=CB[:], in_=cb_d[:, :]), sCONSTP, writes=[bCB])
    for l in range(DEPTH if 'wpool' not in SKIP else 0):
        for g in range(4):
            po = (g % 2) * 64
            R.dma("pool", I("dma_start", out=WPOOL[po:po + 64, l, g // 2, po:po + 64], in_=w_pool_d[l, g]),
                  sCONSTP, writes=[bWPOOL])
    for b in (bCONST, bSK, bVR):
        b.last_w = (sCONST, sCONST.count)
    for b in (bCB, bWPOOL):
        b.last_w = (sCONSTP, sCONSTP.count)

    def prep_list(l):
        groups = [[] for _ in range(NGRP)]

        def tA(t):
            return wscr[l * NTILE + t].rearrange("p (k c) -> p k c", k=8)

        def src(w2d, c0, ncol):
            return w2d[:, c0:c0 + ncol].rearrange("(k p) c -> p k c", p=128)

        def cp(t, dst, s_):
            groups[GRP_OF_TILE[t]].append((dst, s_))

        win = w_in_d[l]
        cp(0, tA(0)[:, :, 0:256], src(win, 0, 256))
        cp(0, tA(0)[:, :, 256:384], src(win, 1024, 128))
        cp(0, tA(0)[:, :, 384:512], src(win, 1536, 128))
        cp(1, tA(1)[:, :, 0:128], src(win, 1280, 128))
        cp(1, tA(1)[:, :, 128:256], src(win, 1152, 128))
        cp(1, tA(1)[:, :, 256:384], src(win, 1664, 128))
        cp(1, tA(1)[:, :, 384:512], src(win, 1408, 128))
        for qc in range(4):
            cp(2, tA(2)[:, :, qc * 128:qc * 128 + 64], src(win, 256 + 64 * qc, 64))
            cp(2, tA(2)[:, :, qc * 128 + 64:qc * 128 + 128], src(win, 256 + 64 * (4 + qc), 64))
        cp(3, tA(3)[:, :, 0:256], src(win, 768, 256))
        for j in range(8):
            t = tA(4 + j)
            for i in range(3):
                cp(4 + j, t[:, :, i * 128:(i + 1) * 128], src(win, 1792 + i * 1024 + j * 128, 128))
            cp(4 + j, t[:, 0:2, 384:512], src(wa_d[l], j * 128, 128))
            cp(4 + j, t[:, 2:6, 384:512], src(wb_d[l], j * 128, 128))
            cp(4 + j, t[:, 6:8, 384:512], src(wc_d[l], j * 128, 128))
        for h in range(2):
            cp(12 + h, tA(12 + h), src(wo_d[l], h * 512, 512))
        for m in range(11):
            cp(14 + m, tA(14 + m)[:, :, 0:256], src(wup_d[l], 256 * m, 256))
            cp(14 + m, tA(14 + m)[:, :, 256:512], src(wup_d[l], DFF + 256 * m, 256))
        for j in range(8):
            dst = wscr[l * NTILE + 25 + j][:, 0:2816].rearrange("p (k c) -> p k c", k=22)
            cp(25 + j, dst, wdn_d[l][:, j * 128:(j + 1) * 128].rearrange("(k p) c -> p k c", p=128))
        return groups

    prep_pending = []

    def prep_queue(l):
        for g, items in enumerate(prep_list(l)):
            for i, (d_, s_) in enumerate(items):
                prep_pending.append((l, g, d_, s_, i == len(items) - 1))

    def prep_issue(n):
        for _ in range(min(n, len(prep_pending))):
            l, g, d_, s_, last = prep_pending.pop(0)
            R.dma("pool", I("dma_start", out=d_, in_=s_), sWSCR[l][g])
            if last:
                bWSCR[l][g].last_w = (sWSCR[l][g], sWSCR[l][g].count)

    def prep_flush_layer(l):
        while any(p[0] <= l for p in prep_pending):
            prep_issue(1)

    if 'prep' not in SKIP:
        prep_queue(0)
        prep_flush_layer(0)

    for g in range(6 if 'vec' not in SKIP else 0):
        k, hp = rPS.alloc()
        pe(I("transpose", out=PSf(k)[:, 0:128], in_=VR[:, g, :], identity=IDF), [bVR, bCONST], [hp])
        act(I("activation", out=VEC[:, g * 128:(g + 1) * 128], in_=PSf(k)[:, 0:128], func=AF.Copy), [hp], [bVEC])
    if 'sk' not in SKIP:
        act(I("activation", out=EXPS[:], in_=SK[:], func=AF.Exp), [bSK], [bEXPS])

    NSEQ = NCH * DEPTH * NTILE
    wstate = {"next": 0}

    def load_ahead(upto):
        upto = min(upto, NSEQ - 1)
        while wstate["next"] <= upto:
            i = wstate["next"]
            wstate["next"] += 1
            l = (i // NTILE) % DEPTH
            t = i % NTILE
            sl = i % nslot
            bWS[sl].gen = i
            ncol = 2816 if t >= 25 else 4096
            if t == 3:
                o_ = WS[sl][:].rearrange("p (k c) -> p k c", k=8)[:, :, 0:256]
                i_ = wscr[l * NTILE + t].rearrange("p (k c) -> p k c", k=8)[:, :, 0:256]
            else:
                o_ = WS[sl][:, 0:ncol]
                i_ = wscr[l * NTILE + t][:, 0:ncol]
            assert bWSCR[l][GRP_OF_TILE[t]].last_w is not None or 'prep' in SKIP, (l, t)
            R.dma("sp", I("dma_start", out=o_, in_=i_), sWS[sl], reads=[bWSCR[l][GRP_OF_TILE[t]]], writes=[bWS[sl]])

    def wtile(c, l, t, hold=0):
        i = (c * DEPTH + l) * NTILE + t
        if c == 0:
            prep_issue(3)
        load_ahead(i + nslot - 1 - hold)
        sl = i % nslot
        return sl, (bWS[sl], i)

    def WA(sl, n, k):
        return WS[sl][:, k * 512 + n * 128:k * 512 + n * 128 + 128]

    def proj(sl, hw, n, s):
        k, hp = rPS.alloc()
        mm_group(PSf(k), [(WA(sl, n, kk), XB[:, kk, tsl(s)]) for kk in range(8)],
                 [hw] + [bXB[kk][s] for kk in range(8)], [hp])
        return k, hp

    def proj4(sl, hw, s):
        banks = [rPS.alloc() for _ in range(4)]
        for kk in range(8):
            pe(IS([("matmul", dict(out=PSf(banks[n][0]), lhsT=WA(sl, n, kk), rhs=XB[:, kk, tsl(s)],
                                   start=(kk == 0), stop=(kk == 7))) for n in range(4)]),
               [hw, bXB[kk][s]], [b_[1] for b_ in banks])
        return banks

    def ln_pre(j, s):
        sl_ = tsl(s)
        act(I("activation", out=XB[:, j, sl_], in_=X[:, j, sl_], func=AF.Copy), [bX[j][s]], [bXB[j][s]])
        act(I("activation", out=LNS[:, j, :], in_=X[:, j, sl_], func=AF.Square), [bX[j][s]], [bLNS[j]])
        if j == 7:
            act(I("activation", out=DEN[:, 1, 4:5], in_=CF[:, 0:1], func=AF.Ln, bias=1.0, scale=1.0), [bCONST], [bDUM])

    def ln_stats(s, js, st):
        sl_ = tsl(s)
        first = st is None
        if first:
            st = (rPS.alloc(), rPS.alloc())
        (km, hm), (kq, hq) = st
        last = (js[-1] == 7)
        pe(IS([("matmul", dict(out=PSf(km), lhsT=ONES, rhs=XB[:, j, sl_], start=(first and j == js[0]),
                               stop=(last and j == 7))) for j in js]), [bCB] + [bXB[j][s] for j in js], [hm])
        pe(IS([("matmul", dict(out=PSf(kq), lhsT=ONES, rhs=LNS[:, j, :], start=(first and j == js[0]),
                               stop=(last and j == 7))) for j in js]), [bCB] + [bLNS[j] for j in js], [hq])
        return st

    def ln_post(l, s, grow, brow, st=None):
        sl_ = tsl(s)
        if st is None:
            st = ln_stats(s, list(range(8)), None)
        else:
            st = ln_stats(s, [7], st)
        (km, hm), (kq, hq) = st
        act(I("activation", out=MEAN[:], in_=PSf(km), func=AF.Copy), [hm], [bMEAN])
        k2, h2 = rFR.alloc()
        V2 = FR[k2][:, 0:512]
        dve(I("tensor_tensor", out=V2, in0=MEAN[:], in1=MEAN[:], op=ALU.mult), [bMEAN], [h2])
        dve(I("tensor_tensor", out=V2, in0=PSf(kq), in1=V2, op=ALU.subtract), [hq, h2], [h2])
        act(I("activation", out=V2, in_=V2, func=AF.Ln, bias=LN_EPS, scale=1.0), [h2], [h2])
        act(I("activation", out=RSTD[:], in_=V2, func=AF.Exp, scale=-0.5), [h2], [bRSTD])
        dve(I("scalar_tensor_tensor", out=MEAN[:], in0=MEAN[:], scalar=-1.0, in1=RSTD[:], op0=ALU.mult, op1=ALU.mult),
            [bMEAN, bRSTD], [bMEAN])
        for j in range(8):
            kx, hx = rFR.alloc()
            XN = FR[kx][:, 0:512]
            dve(I("tensor_tensor", out=XN, in0=X[:, j, sl_], in1=RSTD[:], op=ALU.mult), [bX[j][s], bRSTD], [hx])
            dve(I("tensor_tensor", out=XN, in0=XN, in1=MEAN[:], op=ALU.add), [hx, bMEAN], [hx])
            act(I("activation", out=XB[:, j, sl_], in_=XN, func=AF.Identity, scale=vcol(l, grow + j),
                  bias=vcol(l, brow + j)), [hx, bVEC], [bXB[j][s]])
            act(I("activation", out=X[:, j, sl_], in_=XN, func=AF.Identity, scale=vcol(l, grow + j),
                  bias=vcol(l, brow + j)), [hx, bVEC], [bX[j][s]])

    def prefetch_x(c):
        tok0 = c * T
        for tb in range(NB):
            R.dma("sp", I("dma_start", out=XIN[tb][:], in_=x_d[tok0 + tb * 128:tok0 + (tb + 1) * 128, :]), sXIN[tb],
                  writes=[bXIN[tb]])

    def tables(c):
        tok0 = c * T
        R.dma("sp", I("dma_start", out=POSI[:], in_=pos_d[0:1, tok0:tok0 + T].partition_broadcast(128)), sPOS,
              writes=[bPOSI])
        dve(I("tensor_copy", out=ANG[:], in_=POSI[:]), [bPOSI], [bANG])
        dve(I("tensor_scalar", out=ANG[:], in0=ANG[:], scalar1=INVF, scalar2=None, op0=ALU.mult), [bANG, bCONST], [bANG])
        C1 = 6.28125
        C2 = 2 * PI - 6.28125
        for (TB_, off) in ((ST, 0.0), (CT, 0.5 * PI)):
            dve(I("tensor_scalar", out=TB_[:], in0=ANG[:], scalar1=off, scalar2=None, op0=ALU.add), [bANG], [bTAB])
            dve(I("tensor_scalar", out=POSI[:], in0=TB_[:], scalar1=1.0 / (2 * PI), scalar2=None, op0=ALU.mult),
                [bTAB], [bPOSI])
            dve(I("tensor_copy", out=TMPT[:], in_=POSI[:]), [bPOSI], [bTMPT])
            dve(I("scalar_tensor_tensor", out=TB_[:], in0=TMPT[:], scalar=-C1, in1=TB_[:], op0=ALU.mult, op1=ALU.add),
                [bTMPT, bTAB], [bTAB])
            dve(I("scalar_tensor_tensor", out=TB_[:], in0=TMPT[:], scalar=-C2, in1=TB_[:], op0=ALU.mult, op1=ALU.add),
                [bTMPT, bTAB], [bTAB])
            dve(I("tensor_scalar", out=TB_[:], in0=TB_[:], scalar1=-PI, scalar2=PI, op0=ALU.max, op1=ALU.min),
                [bTAB], [bTAB])
            act(I("activation", out=TB_[:], in_=TB_[:], func=AF.Sin), [bTAB], [bTAB])
        dve(I("tensor_scalar", out=ST[:], in0=ST[:], scalar1=SIGN, scalar2=None, op0=ALU.mult), [bTAB, bCONST], [bTAB])

    def prologue(c):
        for tb in range(NB if 'xin' not in SKIP else 0):
            s = tb // 4
            xs = slice(tb * 128, (tb + 1) * 128)
            for half in range(2):
                k, hp = rPS.alloc()
                pe(IS([("transpose", dict(out=PSf(k)[:, q * 128:(q + 1) * 128],
                                          in_=XIN[tb][:, (half * 4 + q) * 128:(half * 4 + q + 1) * 128], identity=IDF))
                       for q in range(4)]), [bXIN[tb], bCONST], [hp])
                p3 = PSf(k).rearrange("p (q t) -> p q t", q=4)
                act(I("activation", out=X[:, half * 4:half * 4 + 4, xs], in_=p3, func=AF.Copy), [hp],
                    [bX[half * 4 + q][s] for q in range(4)])
                dve(I("tensor_copy", out=XB[:, half * 4:half * 4 + 4, xs], in_=p3), [hp],
                    [bXB[half * 4 + q][s] for q in range(4)])

    def layer(c, l):
        sl0, hw0 = wtile(c, l, 0)
        t0banks = [proj4(sl0, hw0, s) for s in range(NSUB)]
        for s in range(NSUB):
            hpl = bPL[s]
            PL = PLB[s]
            first = (c == 0 and s == 0)
            for cc in range(2):
                kp, hp = t0banks[s][cc]
                ku, hu = rFR.alloc()
                U = FR[ku]
                act(I("activation", out=U[:, 1:16], in_=UPS[l][:, cc, :], func=AF.Copy), [bUPS[l][cc]], [hu])
                act(I("activation", out=U[:, 16:528], in_=PSf(kp), func=AF.Copy), [hp], [hu])
                act(I("activation", out=UPS[l][:, cc, :], in_=U[:, 513:528], func=AF.Copy), [hu], [bUPS[l][cc]])
                k1, h1 = rFR.alloc()
                A1 = FR[k1]
                pool(I("tensor_tensor", out=A1[:, 2:528], in0=U[:, 2:528], in1=U[:, 1:527], op=ALU.add), [hu], [h1])
                k2, h2 = rFR.alloc()
                A2 = FR[k2]
                pool(I("tensor_tensor", out=A2[:, 4:528], in0=A1[:, 4:528], in1=A1[:, 2:526], op=ALU.add), [h1], [h2])
                if cc == 0:
                    grp = [(0, A1, h1, 0.5), (64, A2, h2, 0.25)]
                else:
                    k3, h3 = rFR.alloc()
                    A3 = FR[k3]
                    pool(I("tensor_tensor", out=A3[:, 8:528], in0=A2[:, 8:528], in1=A2[:, 4:524], op=ALU.add),
                         [h2], [h3])
                    k4, h4 = rFR.alloc()
                    A4 = FR[k4]
                    pool(I("tensor_tensor", out=A4[:, 16:528], in0=A3[:, 16:528], in1=A3[:, 8:520], op=ALU.add),
                         [h3], [h4])
                    grp = [(0, A3, h3, 0.125), (64, A4, h4, 0.0625)]
                for (p0, A, hA, inv) in grp:
                    ps_ = slice(p0, p0 + 64)
                    dve(I("scalar_tensor_tensor", out=PL[ps_, cc, :], in0=A[ps_, 16:528], scalar=inv,
                          in1=U[ps_, 16:528], op0=ALU.mult, op1=ALU.subtract), [hA, hu], [hpl])
                    if first:
                        kf, hf = rFR.alloc()
                        pool(I("tensor_tensor", out=FR[kf][ps_, 0:15], in0=A[ps_, 16:31], in1=INVC[ps_, cc, 0:15],
                               op=ALU.mult), [hA, bCONST], [hf])
                        pool(I("tensor_tensor", out=PL[ps_, cc, 0:15], in0=FR[kf][ps_, 0:15], in1=U[ps_, 16:31],
                               op=ALU.subtract), [hf, hu], [hpl])

        if DBG <= 1:
            return
        sl1, hw1 = wtile(c, l, 1, hold=1)
        for s in range(NSUB):
            for h in range(2):
                (sx, hx_, nx) = (sl0, hw0, 2) if h == 0 else (sl1, hw1, 1)
                (sg, hg_, ng) = (sl0, hw0, 3) if h == 0 else (sl1, hw1, 2)
                nb_ = 0 if h == 0 else 3
                if h == 0:
                    (kxc, hxc), (kgc, hgc) = t0banks[s][2], t0banks[s][3]
                else:
                    kxc, hxc = proj(sx, hx_, nx, s)
                    kgc, hgc = proj(sg, hg_, ng, s)
                kgb, hgb = proj(sl1, hw1, nb_, s)
                kx, hx = rFR.alloc()
                XC = FR[kx][:, 0:512]
                act(I("activation", out=XC, in_=PSf(kxc), func=AF.Copy), [hxc], [hx])
                kpp, hpp = rFR.alloc()
                PP = FR[kpp]
                dve(I("tensor_copy", out=PP[:, 0:2], in_=PPS[l][:, h, :]), [bPPS[l][h]], [hpp])
                dve(I("tensor_tensor", out=PP[:, 2:514], in0=PSf(kgc), in1=XC, op=ALU.mult), [hgc, hx], [hpp])
                dve(I("tensor_copy", out=PPS[l][:, h, :], in_=PP[:, 512:514]), [hpp], [bPPS[l][h]])
                ky, hy = rFR.alloc()
                Y = FR[ky][:, 0:512]
                act(I("activation", out=Y, in_=PP[:, 0:512], func=AF.Identity, scale=vcol(l, 164 + h)),
                    [hpp, bVEC], [hy])
                dve(I("scalar_tensor_tensor", out=Y, in0=PP[:, 1:513], scalar=vcol(l, 166 + h), in1=Y,
                      op0=ALU.mult, op1=ALU.add), [hpp, hy, bVEC], [hy])
                dve(I("scalar_tensor_tensor", out=Y, in0=PP[:, 2:514], scalar=vcol(l, 168 + h), in1=Y,
                      op0=ALU.mult, op1=ALU.add), [hpp, hy, bVEC], [hy])
                dve(I("tensor_tensor", out=OCv(h, tsl(s)), in0=PSf(kgb), in1=Y, op=ALU.mult), [hgb, hy], [bOC[h][s]])

        if DBG <= 2:
            return
        sl2, hw2 = wtile(c, l, 2)
        sl3, hw3 = wtile(c, l, 3, hold=1)

        def rope_rot(qc, s, QB, hb_, T1, h1):
            kr, hr = rPS.alloc()
            mm_group(PSf(kr), [(RM, QB)], [bCB, hb_], [hr])
            k2, h2 = rFR.alloc()
            T2 = FR[k2][:, 0:512]
            dve(I("tensor_tensor", out=T2, in0=PSf(kr), in1=ST[:, tsl(s)], op=ALU.mult), [hr, bTAB], [h2])
            if qc < 4:
                dst = QTv(qc, tsl(s))
                wr = [bQT[qc][s]]
            else:
                dst = KT[l][:, 128 + s * 512:128 + (s + 1) * 512]
                wr = [bKT[l][s + 1]]
            pool(I("tensor_tensor", out=dst, in0=T1, in1=T2, op=ALU.add), [h1, h2], wr)

        for s in range(NSUB):
            pend = None
            for qc in (4, 0, 1, 2, 3):
                if qc < 4:
                    kq, hq = proj(sl2, hw2, qc, s)
                else:
                    kq, hq = proj(sl3, hw3, 0, s)
                kb_, hb_ = rBR.alloc()
                QB = BR[kb_][:, 0:512]
                act(I("activation", out=QB, in_=PSf(kq), func=AF.Copy), [hq], [hb_])
                k1, h1 = rFR.alloc()
                T1 = FR[k1][:, 0:512]
                dve(I("tensor_tensor", out=T1, in0=PSf(kq), in1=CT[:, tsl(s)], op=ALU.mult), [hq, bTAB], [h1])
                if pend is not None:
                    rope_rot(*pend)
                pend = (qc, s, QB, hb_, T1, h1)
            kv, hv = rPS.alloc()
            items = []
            for blk in range(4):
                ts_ = slice(s * 512 + blk * 128, s * 512 + (blk + 1) * 128)
                for kk in range(8):
                    items.append(("matmul", dict(out=PSf(kv)[:, blk * 128:(blk + 1) * 128], lhsT=XB[:, kk, ts_],
                                                 rhs=WA(sl3, 1, kk), start=(kk == 0), stop=(kk == 7))))
            pe(IS(items), [hw3] + [bXB[kk][s] for kk in range(8)], [hv])
            rope_rot(*pend)
            for blk in range(4):
                act(I("activation", out=VT[l][:, 1 + s * 4 + blk, :, 0:64],
                      in_=PSf(kv)[:, blk * 128:(blk + 1) * 128].rearrange("p (g d) -> p g d", g=2), func=AF.Copy),
                    [hv], [bVT[l][s + 1]])
        if DBG <= 3:
            return

        def attn_scores(n):
            s = n // 4
            kbs = [n - 1, n]
            if c == 0 and n == 0:
                kbs = [n]
            plist = []
            for g in range(2):
                gp = slice(g * 64, (g + 1) * 64)
                for kb in kbs:
                    sk = 0 if kb < 0 else 1 + kb // 4
                    kst, hst = rPS.alloc()
                    M = MCUR if kb == n else MPREV
                    pe(IS([("matmul", dict(out=PSf(kst), lhsT=KT[l][gp, (kb + 1) * 128:(kb + 2) * 128],
                                           rhs=HR[gp, 16:20, n * 128:(n + 1) * 128], start=True, stop=False)),
                           ("matmul", dict(out=PSf(kst), lhsT=IDB, rhs=M, start=False, stop=True))]),
                       [bKT[l][sk], bCB] + [bQT[q][s] for q in range(4)], [hst])
                    kp_, hp_ = rPB.alloc()
                    P = PB[kp_][:]
                    act(I("activation", out=P, in_=PSf(kst), func=AF.Exp, scale=0.125), [hst], [hp_])
                    plist.append((g, kb, sk, P, hp_))
            return plist, kbs

        def attn_pv(n, plist, kbs):
            s = n // 4
            ko = [rPS.alloc(), rPS.alloc()]
            for (g, kb, sk, P, hp_) in plist:
                pe(IS([("matmul", dict(out=PSf(ko[g][0])[:, hh * 65:(hh + 1) * 65],
                                       lhsT=P[:, hh * 128:(hh + 1) * 128], rhs=VT[l][:, kb + 1, g, :],
                                       start=(kb == kbs[0] and hh == 0), stop=(kb == kbs[-1] and hh == 3),
                                       skip_group_check=True)) for hh in range(4)]),
                   [hp_, bVT[l][sk]], [ko[g][1]])
            kob, hob = rBR.alloc()
            OBT = BR[kob][:, 0:512]
            for g in range(2):
                O3 = PSf(ko[g][0])[:, 0:260].rearrange("p (h d) -> p h d", h=4)
                dve(I("tensor_tensor", out=DEN[:, g, 0:4], in0=O3[:, :, 64],
                      in1=EXPS[:, l * 8 + g * 4:l * 8 + g * 4 + 4], op=ALU.add), [ko[g][1], bEXPS], [bDEN])
                dve(I("reciprocal", out=RDEN[:, g, 0:4], in_=DEN[:, g, 0:4]), [bDEN], [bRDEN])
                dve(I("tensor_tensor", out=OBT[:, g * 256:(g + 1) * 256].rearrange("p (h d) -> p h d", h=4),
                      in0=O3[:, :, 0:64], in1=RDEN[:, g, 0:4].unsqueeze(2).broadcast_to([128, 4, 64]), op=ALU.mult),
                    [ko[g][1], bRDEN], [hob])
            ktp, htp = rPS.alloc()
            pe(IS([("transpose", dict(out=PSb(ktp)[:, cq * 128:(cq + 1) * 128], in_=OBT[:, cq * 128:(cq + 1) * 128],
                                      identity=IDB)) for cq in range(4)]), [hob, bCB], [htp])
            dve(I("tensor_copy", out=HR[:, 8:12, n * 128:(n + 1) * 128],
                  in_=PSb(ktp)[:, 0:512].rearrange("p (q t) -> p q t", q=4)),
                [htp], [bOB[q][s] for q in range(4)])

        pend = None
        for n in range(NB):
            cur = attn_scores(n)
            if pend is not None:
                attn_pv(*pend)
            pend = (n,) + cur
        attn_pv(*pend)
        if l == DEPTH - 1 and c + 1 < NCH and 'tables' not in SKIP:
            tables(c + 1)
        pool(I("tensor_copy", out=KT[l][:, 0:128], in_=KT[l][:, NB * 128:(NB + 1) * 128]), [bKT[l][NSUB]], [bKT[l][0]])
        pool(I("tensor_copy", out=VT[l][:, 0, :, :], in_=VT[l][:, NB, :, :]), [bVT[l][NSUB]], [bVT[l][0]])

        if DBG <= 4:
            return
        for s in range(NSUB):
            for cc in range(2):
                km, hm = rPS.alloc()
                mm_group(PSf(km), [(WPOOL[:, l, cc, :], PLB[s][:, cc, :])], [bWPOOL, bPL[s]], [hm])
                act(I("activation", out=OAv(cc, tsl(s)), in_=PSf(km), func=AF.Identity, scale=vcol(l, 170 + cc)),
                    [hm, bVEC], [bOA[cc][s]])
        for j in range(8):
            slj, hwj = wtile(c, l, 4 + j)
            for s in range(NSUB):
                srcs = [(0, 2, [OAv(kk, tsl(s)) for kk in range(2)], [bOA[0][s], bOA[1][s]]),
                        (2, 6, [OBv(kk, tsl(s)) for kk in range(4)], [bOB[q][s] for q in range(4)]),
                        (6, 8, [OCv(kk, tsl(s)) for kk in range(2)], [bOC[0][s], bOC[1][s]])]
                ms = []
                for i in range(3):
                    kg, hg = proj(slj, hwj, i, s)
                    k0, k1_, aps, rb = srcs[i]
                    kbp, hbp = rPS.alloc()
                    mm_group(PSf(kbp), [(WA(slj, 3, kk), aps[kk - k0]) for kk in range(k0, k1_)], [hwj] + rb, [hbp])
                    ksg, hsg = rFR.alloc()
                    SG = FR[ksg][:, 0:512]
                    act(I("activation", out=SG, in_=PSf(kg), func=AF.Sigmoid), [hg], [hsg])
                    dve(I("tensor_tensor", out=SG, in0=PSf(kbp), in1=SG, op=ALU.mult), [hbp, hsg], [hsg])
                    ms.append((SG, hsg))
                pool(I("tensor_tensor", out=ms[0][0], in0=ms[0][0], in1=ms[1][0], op=ALU.add),
                     [ms[0][1], ms[1][1]], [ms[0][1]])
                pool(I("tensor_tensor", out=MGv(j, tsl(s)), in0=ms[0][0], in1=ms[2][0], op=ALU.add),
                     [ms[0][1], ms[2][1]], [bMG[j][s]])

        if DBG <= 5:
            return
        slo = [wtile(c, l, 12), wtile(c, l, 13, hold=1)]
        for s in range(NSUB):
            for j in range(8):
                so, ho = slo[j // 4]
                km, hm = rPS.alloc()
                mm_group(PSf(km), [(WA(so, j % 4, kk), MGv(kk, tsl(s))) for kk in range(8)],
                         [ho] + [bMG[kk][s] for kk in range(8)], [hm])
                if j == 7:
                    st = ln_stats(s, list(range(7)), None)
                dve(I("scalar_tensor_tensor", out=X[:, j, tsl(s)], in0=X[:, j, tsl(s)], scalar=ALPHA, in1=PSf(km),
                      op0=ALU.mult, op1=ALU.add), [hm, bX[j][s]], [bX[j][s]])
                ln_pre(j, s)
            ln_post(l, s, 132, 140, st)

        if DBG <= 6:
            return
        for m in range(11):
            slm, hwm = wtile(c, l, 14 + m)
            for s in range(NSUB):
                mb = proj4(slm, hwm, s) if (m == 0 and NSUB == 1) else None
                for i in range(2):
                    fa = 2 * m + i
                    fb = 22 + fa
                    ys = []
                    for (f, n_) in ((fa, i), (fb, 2 + i)):
                        kp, hp = mb[n_] if mb is not None else proj(slm, hwm, n_, s)
                        P = PSf(kp)
                        HL = UST[l][:, f, :]
                        ky, hy = rFR.alloc()
                        Y = FR[ky][:, 0:512]
                        w0, w1, w2 = vcol(l, f), vcol(l, 44 + f), vcol(l, 88 + f)
                        act(I("activation", out=Y[:, 2:512], in_=P[:, 0:510], func=AF.Identity, scale=w0),
                            [hp, bVEC], [hy])
                        act(I("activation", out=Y[:, 0:2], in_=HL, func=AF.Identity, scale=w0),
                            [bUST[l][f], bVEC], [hy])
                        dve(I("scalar_tensor_tensor", out=Y[:, 0:1], in0=HL[:, 1:2], scalar=w1, in1=Y[:, 0:1],
                              op0=ALU.mult, op1=ALU.add), [bUST[l][f], hy, bVEC], [hy])
                        dve(I("scalar_tensor_tensor", out=Y[:, 1:512], in0=P[:, 0:511], scalar=w1, in1=Y[:, 1:512],
                              op0=ALU.mult, op1=ALU.add), [hp, hy, bVEC], [hy])
                        dve(I("scalar_tensor_tensor", out=Y, in0=P, scalar=w2, in1=Y,
                              op0=ALU.mult, op1=ALU.add), [hp, hy, bVEC], [hy])
                        dve(I("tensor_copy", out=HL, in_=P[:, 510:512]), [hp], [bUST[l][f]])
                        ys.append((Y, hy))
                    (YA, hya), (YB, hyb) = ys
                    act(I("activation", out=YA, in_=YA, func=AF.Silu), [hya], [hya])
                    pool(I("tensor_tensor", out=HR[:, fa, tsl(s)], in0=YA, in1=YB, op=ALU.mult), [hya, hyb], [bH[fa][s]])

        if DBG <= 7:
            return
        for j in range(8):
            sld, hwd = wtile(c, l, 25 + j)
            for s in range(NSUB):
                kd, hd = rPS.alloc()
                if j == 0:
                    for kk in range(22):
                        pe(I("matmul", out=PSf(kd), lhsT=WS[sld][:, kk * 128:(kk + 1) * 128], rhs=HR[:, kk, tsl(s)],
                             start=(kk == 0), stop=(kk == 21)), [hwd, bH[kk][s]], [hd])
                else:
                    mm_group(PSf(kd), [(WS[sld][:, kk * 128:(kk + 1) * 128], HR[:, kk, tsl(s)]) for kk in range(22)],
                             [hwd] + [bH[kk][s] for kk in range(22)], [hd])
                st2 = None
                if NSUB == 1 and j == 7:
                    st2 = ln_stats(s, list(range(7)), None)
                dve(I("scalar_tensor_tensor", out=X[:, j, tsl(s)], in0=X[:, j, tsl(s)], scalar=ALPHA, in1=PSf(kd),
                      op0=ALU.mult, op1=ALU.add), [hd, bX[j][s]], [bX[j][s]])
                if NSUB == 1:
                    ln_pre(j, s)
        for s in range(NSUB):
            if NSUB > 1:
                for j in range(8):
                    ln_pre(j, s)
            ln_post(l, s, 148, 156, st2 if NSUB == 1 else None)

    def epilogue(c):
        tok0 = c * T
        for tb in range(NB):
            s = tb // 4
            ki, hi = rIO.alloc()
            for half in range(2 if 'epi' not in SKIP else 0):
                k, hp = rPS.alloc()
                pe(IS([("transpose", dict(out=PSf(k)[:, q * 128:(q + 1) * 128],
                                          in_=X[:, half * 4 + q, tb * 128:(tb + 1) * 128], identity=IDF))
                       for q in range(4)]), [bX[half * 4 + q][s] for q in range(4)] + [bCONST], [hp])
                if half == 0:
                    act(I("activation", out=IO[ki][:, 0:512], in_=PSf(k), func=AF.Copy), [hp], [hi])
                else:
                    dve(I("tensor_copy", out=IO[ki][:, 512:1024], in_=PSf(k)), [hp], [hi])
            R.dma("sp", I("dma_start", out=out_d[tok0 + tb * 128:tok0 + (tb + 1) * 128, :], in_=IO[ki][:]), sIO[ki],
                  reads=[hi])

    prefetch_x(0)
    if 'tables' not in SKIP:
        tables(0)
    for c in range(NCH):
        prologue(c)
        for l in range(DEPTH):
            if l == DEPTH - 1 and c + 1 < NCH:
                prefetch_x(c + 1)
            if c == 0 and l + 1 < DEPTH and 'prep' not in SKIP:
                prep_queue(l + 1)
            if DBG > 0:
                layer(c, l)
            if c == 0 and l + 1 < DEPTH and 'prep' not in SKIP:
                prep_flush_layer(l + 1)
        epilogue(c)

    for i in range(2):
        R.wait_only("sp", (sIO[i], sIO[i].count))

    for s_ in R.sems:
        s_.h = nc.alloc_semaphore(s_.name)

    def run(eng, lst):
        for waits, fn, sem, step in lst:
            if fn is None:
                for (ws, wv) in waits:
                    eng.wait_ge(ws.h, wv)
                continue
            for (ws, wv) in waits[:-1]:
                eng.wait_ge(ws.h, wv)
            r = fn(eng)
            first, last = r if isinstance(r, tuple) else (r, r)
            if waits:
                first._wait_ge(waits[-1][0].h, waits[-1][1])
            last.then_inc(sem.h, step)

    with nc.Block() as block:
        @block.tensor
        def _(e):
            run(e, R.ops["pe"])

        @block.scalar
        def _(e):
            run(e, R.ops["act"])

        @block.vector
        def _(e):
            run(e, R.ops["dve"])

        @block.gpsimd
        def _(e):
            run(e, R.ops["pool"])

        @block.sync
        def _(e):
            run(e, R.ops["sp"])
    stats = {e: len(v) for e, v in R.ops.items()}
    stats["sem_counts"] = {s_.name: s_.count for s_ in R.sems}
    return nc, stats


def make_consts():
    cf = np.zeros((128, 192), np.float32)
    cf[:, 0:128] = np.eye(128, dtype=np.float32)
    inv_freq = (np.float32(500000.0) ** (-np.arange(0, 16, 2, dtype=np.float32) / np.float32(16))).astype(np.float32)
    for p in range(128):
        r = p % 64
        if r < 16:
            cf[p, 128] = inv_freq[r % 8]
            cf[p, 129] = -1.0 if r < 8 else 1.0
        for cc in range(2):
            w = (2, 4, 8, 16)[cc * 2 + (p // 64)]
            for t in range(16):
                cf[p, 130 + cc * 16 + t] = 1.0 / min(t + 1, w)
    cb = np.zeros((128, 1408), np.float32)
    cb[:, 0:128] = np.eye(128, dtype=np.float32)
    for m in range(128):
        r = m % 64
        if r < 8:
            cb[m + 8, 128 + m] = 1.0
        elif r < 16:
            cb[m - 8, 128 + m] = 1.0
    cb[:, 256:384] = 1.0 / 1024.0
    jj = np.arange(128)[:, None]
    ii = np.arange(128)[None, :]
    cur = np.where(jj <= ii, 0.0, -30000.0).astype(np.float32)
    prev = np.where(jj > ii, 0.0, -30000.0).astype(np.float32)
    cb[:, 384:896] = np.tile(cur, (1, 4))
    cb[:, 896:1408] = np.tile(prev, (1, 4))
    return cf, cb


def make_vecs(depth, ffn_conv_w, ln1_g, ln1_b, ln2_g, ln2_b, conv_w, pool_scale):
    rows = []
    for l in range(depth):
        rows.append(np.asarray(ffn_conv_w[l], np.float32).reshape(132, 128))
        rows.append(np.asarray(ln1_g[l], np.float32).reshape(8, 128))
        rows.append(np.asarray(ln1_b[l], np.float32).reshape(8, 128))
        rows.append(np.asarray(ln2_g[l], np.float32).reshape(8, 128))
        rows.append(np.asarray(ln2_b[l], np.float32).reshape(8, 128))
        rows.append(np.asarray(conv_w[l], np.float32).reshape(6, 128))
        rows.append(np.asarray(pool_scale[l], np.float32).reshape(2, 128))
    v = np.concatenate(rows, axis=0)
    out = np.zeros((768, 128), np.float32)
    out[:v.shape[0]] = v
    return out


_CACHE = {}


def run_model(inputs, S, DEPTH, T, n_cores, trace=False):
    key = (S, DEPTH, T)
    if key not in _CACHE:
        _CACHE[key] = build(S, DEPTH, T)
    nc, stats = _CACHE[key]
    cf, cb = make_consts()
    f = lambda k: np.ascontiguousarray(np.asarray(inputs[k], np.float32)[:DEPTH])
    vecs = make_vecs(DEPTH, inputs["ffn_conv_w"], inputs["ln1_g"], inputs["ln1_b"], inputs["ln2_g"], inputs["ln2_b"],
                     inputs["conv_w"], inputs["pool_scale"])
    sinks = np.zeros((1, 32), np.float32)
    sinks[0, :DEPTH * 8] = np.asarray(inputs["attn_sinks"], np.float32)[:DEPTH].reshape(-1)
    shared = {
        "w_in": f("w_in"), "w_pool": f("w_pool"), "w_branch_a": f("w_branch_a"), "w_branch_b": f("w_branch_b"),
        "w_branch_c": f("w_branch_c"), "w_o": f("w_o"), "w_up": f("w_up"), "w_down": f("w_down"),
        "vecs": vecs, "sinks": sinks, "cf32": cf, "cbf": cb,
    }
    x = np.asarray(inputs["x"], np.float32)
    pos = np.asarray(inputs["positions"], np.int32)
    in_maps = []
    for b in range(n_cores):
        m = dict(shared)
        m["x"] = np.ascontiguousarray(x[b, :S])
        m["pos"] = np.ascontiguousarray(pos[b, :S]).reshape(1, S)
        in_maps.append(m)
    res = run_bass_kernel_spmd(nc, in_maps, core_ids=list(range(n_cores)), trace=trace)
    out = np.stack([np.asarray(r["out"], np.float32) for r in res.results], axis=0)
    return out, res


def kernel(**inputs):
    out, _ = run_model(inputs, 8192, 4, 512, 8)
    return out
```

```python
import math
import os
import numpy as np
import concourse.bass as bass
import concourse.mybir as mybir
from concourse.bass_utils import run_bass_kernel_spmd

F32 = mybir.dt.float32
BF16 = mybir.dt.bfloat16
I32 = mybir.dt.int32
AF = mybir.ActivationFunctionType
ALU = mybir.AluOpType

D = 1024
IN_W = 4864
DFF = 2816
NTILE = 33
ALPHA = 8.0 ** 0.25
LN_EPS = 1e-5
VROWS_L = 172
PI = math.pi
DBG = int(os.environ.get('KDBG', '99'))
SKIP = os.environ.get('KSKIP', '').split(',')


class Sem:
    def __init__(self, name):
        self.name = name
        self.count = 0
        self.h = None


class Buf:
    __slots__ = ("name", "last_w", "readers", "gen", "aliases", "excl")

    def __init__(self, name, excl=False):
        self.name = name
        self.excl = excl
        self.last_w = None
        self.readers = {}
        self.gen = 0
        self.aliases = []


def _b(h):
    if isinstance(h, tuple):
        buf, gen = h
        assert buf.gen == gen, f"stale ring buffer {buf.name}: gen {gen} != {buf.gen}"
        return buf
    return h


class Rec:
    ENGS = ("pe", "act", "dve", "pool", "sp")

    def __init__(self):
        self.ops = {e: [] for e in self.ENGS}
        self.esem = {e: Sem("e_" + e) for e in self.ENGS}
        self.know = {e: {} for e in self.ENGS}
        self.sems = list(self.esem.values())
        self.hist = {s: {} for s in self.sems}

    def new_sem(self, name):
        s = Sem(name)
        self.sems.append(s)
        self.hist[s] = {}
        return s

    def _collect(self, reads, writes, own):
        deps = {}

        def add(d):
            if d is None:
                return
            s, v = d
            if deps.get(s, 0) < v:
                deps[s] = v

        for h in reads:
            b = _b(h)
            add(b.last_w)
            if b.excl:
                for s, v in b.readers.items():
                    if s is not own:
                        add((s, v))
        for h in writes:
            b = _b(h)
            for bb in [b] + b.aliases:
                add(bb.last_w)
                for s, v in bb.readers.items():
                    add((s, v))
        return deps

    def _emit(self, e, fn, reads, writes, sem, step):
        deps = self._collect(reads, writes, sem)
        know = self.know[e]
        cand = []
        for s, v in deps.items():
            if e == "pe" and s is self.esem["pe"]:
                continue
            if know.get(s, 0) >= v:
                continue
            cand.append((s, v))
        waits = []
        for i, (s, v) in enumerate(cand):
            covered = False
            for j, (s2, v2) in enumerate(cand):
                if j == i:
                    continue
                k2 = self.hist[s2].get(v2)
                if k2 is not None and k2.get(s, 0) >= v:
                    k1 = self.hist[s].get(v)
                    if k1 is not None and k1.get(s2, 0) >= v2 and i < j:
                        continue
                    covered = True
                    break
            if not covered:
                waits.append((s, v))
        for (s, v) in waits:
            ks = self.hist[s].get(v)
            if ks is not None:
                for a, b in ks.items():
                    if know.get(a, 0) < b:
                        know[a] = b
            if know.get(s, 0) < v:
                know[s] = v
        sem.count += step
        me = (sem, sem.count)
        snap = dict(know)
        snap[sem] = sem.count
        self.hist[sem][sem.count] = snap
        self.ops[e].append((waits, fn, sem, step))
        for h in reads:
            b = _b(h)
            if b.readers.get(sem, 0) < sem.count:
                b.readers[sem] = sem.count
        for h in writes:
            b = _b(h)
            b.last_w = me
            b.readers = {}
        return me

    def op(self, e, fn, reads=(), writes=()):
        return self._emit(e, fn, reads, writes, self.esem[e], 1)

    def dma(self, q, fn, dsem, reads=(), writes=()):
        return self._emit(q, fn, reads, writes, dsem, 16)

    def wait_only(self, e, dep):
        s, v = dep
        if self.know[e].get(s, 0) >= v:
            return
        self.know[e][s] = v
        self.ops[e].append(([(s, v)], None, None, 0))


class Ring:
    def __init__(self, name, n, excl=False):
        self.bufs = [Buf(f"{name}{i}", excl) for i in range(n)]
        self.n = n
        self.i = 0

    def alloc(self):
        k = self.i % self.n
        self.i += 1
        b = self.bufs[k]
        b.gen += 1
        return k, (b, b.gen)


def build(S, DEPTH, T, nslot=5, nf32=12, nbf=6):
    NSUB = T // 512
    NB = T // 128
    NCH = S // T
    assert S % T == 0 and T % 512 == 0

    nc = bass.Bass("TRN2", target_bir_lowering=False)
    R = Rec()

    def dram_in(name, shape, dt=F32):
        return nc.dram_tensor(name, list(shape), dt, kind="ExternalInput").ap()

    x_d = dram_in("x", [S, D])
    pos_d = dram_in("pos", [1, S], I32)
    w_in_d = dram_in("w_in", [DEPTH, D, IN_W])
    w_pool_d = dram_in("w_pool", [DEPTH, 4, 64, 64])
    wa_d = dram_in("w_branch_a", [DEPTH, 256, D])
    wb_d = dram_in("w_branch_b", [DEPTH, 512, D])
    wc_d = dram_in("w_branch_c", [DEPTH, 256, D])
    wo_d = dram_in("w_o", [DEPTH, D, D])
    wup_d = dram_in("w_up", [DEPTH, D, 2 * DFF])
    wdn_d = dram_in("w_down", [DEPTH, DFF, D])
    vecs_d = dram_in("vecs", [768, 128])
    sinks_d = dram_in("sinks", [1, 32])
    cf_d = dram_in("cf32", [128, 192])
    cb_d = dram_in("cbf", [128, 1408])
    out_d = nc.dram_tensor("out", [S, D], F32, kind="ExternalOutput").ap()
    wscr = nc.dram_tensor("wscr", [DEPTH * NTILE, 128, 4096], BF16).ap()

    def sb(name, shape, dt):
        return nc.alloc_sbuf_tensor(name, list(shape), dt)

    X = sb("X", [128, 8, T], F32)
    XB = sb("XB", [128, 8, T], BF16)
    HR = sb("HR", [128, 22, T], BF16)
    WS = [sb(f"WS{i}", [128, 4096], BF16) for i in range(nslot)]
    CF = sb("CF", [128, 192], F32)
    CB = sb("CB", [128, 1408], BF16)
    VR = sb("VR", [128, 6, 128], F32)
    VEC = sb("VEC", [128, 768], F32)
    WPOOL = sb("WPOOL", [128, DEPTH, 2, 128], BF16)
    SK = sb("SK", [128, 32], F32)
    EXPS = sb("EXPS", [128, 32], F32)
    CT = sb("CT", [128, T], F32)
    ST = sb("ST", [128, T], F32)
    POSI = sb("POSI", [128, T], I32)
    ANG = sb("ANG", [128, T], F32)
    TMPT = sb("TMPT", [128, T], F32)
    KT = [sb(f"KT{l}", [128, (NB + 1) * 128], BF16) for l in range(DEPTH)]
    VT = [sb(f"VT{l}", [128, NB + 1, 2, 65], BF16) for l in range(DEPTH)]
    UPS = [sb(f"UPS{l}", [128, 2, 15], F32) for l in range(DEPTH)]
    PPS = [sb(f"PPS{l}", [128, 2, 2], F32) for l in range(DEPTH)]
    UST = [sb(f"UST{l}", [128, 44, 2], F32) for l in range(DEPTH)]
    LNS = sb("LNS", [128, 8, 512], BF16)
    MEAN = sb("MEAN", [128, 512], F32)
    RSTD = sb("RSTD", [128, 512], F32)
    DEN = sb("DEN", [128, 2, 8], F32)
    RDEN = sb("RDEN", [128, 2, 8], F32)
    FR = [sb(f"FR{i}", [128, 528], F32) for i in range(nf32)]
    BR = [sb(f"BR{i}", [128, 1024], BF16) for i in range(nbf)]
    PLB = [sb(f"PLB{i}", [128, 2, 512], BF16) for i in range(NSUB)]
    PB = [sb(f"PB{i}", [128, 512], BF16) for i in range(10)]
    IO = [sb(f"IO{i}", [128, 1024], F32) for i in range(2)]
    XIN = [sb(f"XIN{i}", [128, 1024], F32) for i in range(NB)]
    PS = nc.alloc_psum_tensor("PS", [128, 8, 512], F32)

    def MGv(j, sl):
        return HR[:, j, sl]

    def OBv(cq, sl):
        return HR[:, 8 + cq, sl]

    def OAv(cc, sl):
        return HR[:, 12 + cc, sl]

    def OCv(h, sl):
        return HR[:, 14 + h, sl]

    def QTv(qc, sl):
        return HR[:, 16 + qc, sl]

    bX = [[Buf(f"X{j}_{s}") for s in range(NSUB)] for j in range(8)]
    bXB = [[Buf(f"XB{j}_{s}") for s in range(NSUB)] for j in range(8)]
    bH = [[Buf(f"H{f}_{s}") for s in range(NSUB)] for f in range(22)]
    bMG = [[Buf(f"MG{j}_{s}") for s in range(NSUB)] for j in range(8)]
    bOB = [[Buf(f"OB{j}_{s}") for s in range(NSUB)] for j in range(4)]
    bOA = [[Buf(f"OA{j}_{s}") for s in range(NSUB)] for j in range(2)]
    bOC = [[Buf(f"OC{j}_{s}") for s in range(NSUB)] for j in range(2)]
    bQT = [[Buf(f"QT{j}_{s}") for s in range(NSUB)] for j in range(4)]
    for s in range(NSUB):
        for j in range(8):
            bMG[j][s].aliases.append(bH[j][s]); bH[j][s].aliases.append(bMG[j][s])
        for j in range(4):
            bOB[j][s].aliases.append(bH[8 + j][s]); bH[8 + j][s].aliases.append(bOB[j][s])
            bQT[j][s].aliases.append(bH[16 + j][s]); bH[16 + j][s].aliases.append(bQT[j][s])
        for j in range(2):
            bOA[j][s].aliases.append(bH[12 + j][s]); bH[12 + j][s].aliases.append(bOA[j][s])
            bOC[j][s].aliases.append(bH[14 + j][s]); bH[14 + j][s].aliases.append(bOC[j][s])
    bWS = [Buf(f"WS{i}") for i in range(nslot)]
    sWS = [R.new_sem(f"ws{i}") for i in range(nslot)]
    bCONST = Buf("const")
    sCONST = R.new_sem("const")
    sCONSTP = R.new_sem("constp")
    bCB = Buf("CB")
    bVR = Buf("VR")
    bVEC = Buf("VEC")
    bWPOOL = Buf("WPOOL")
    bEXPS = Buf("EXPS")
    bSK = Buf("SK")
    bTAB = Buf("TAB")
    bPOSI = Buf("POSI")
    sPOS = R.new_sem("pos")
    bANG = Buf("ANG")
    bTMPT = Buf("TMPT")
    bKT = [[Buf(f"KT{l}_{s}") for s in range(NSUB + 1)] for l in range(DEPTH)]
    bVT = [[Buf(f"VT{l}_{s}") for s in range(NSUB + 1)] for l in range(DEPTH)]
    bUPS = [[Buf(f"UPS{l}_{c}") for c in range(2)] for l in range(DEPTH)]
    bPPS = [[Buf(f"PPS{l}_{c}") for c in range(2)] for l in range(DEPTH)]
    bUST = [[Buf(f"UST{l}_{f}") for f in range(44)] for l in range(DEPTH)]
    bLNS = [Buf(f"LNS{j}") for j in range(8)]
    bMEAN = Buf("MEAN")
    bRSTD = Buf("RSTD")
    bDEN = Buf("DEN")
    bDUM = Buf("DUM")
    bRDEN = Buf("RDEN")
    rFR = Ring("FR", nf32)
    rBR = Ring("BR", nbf)
    rPB = Ring("PB", 10)
    bPL = [Buf(f"PL{i}") for i in range(NSUB)]
    rPS = Ring("PS", 8, excl=True)
    rIO = Ring("IO", 2)
    sIO = [R.new_sem(f"io{i}") for i in range(2)]
    bXIN = [Buf(f"XIN{i}") for i in range(NB)]
    sXIN = [R.new_sem(f"xin{i}") for i in range(NB)]
    NGRP = 6
    GRP_OF_TILE = [0] * 4 + [1] * 4 + [2] * 4 + [3] * 6 + [4] * 7 + [5] * 8
    assert len(GRP_OF_TILE) == NTILE
    bWSCR = [[Buf(f"wscr{l}_{g}") for g in range(NGRP)] for l in range(DEPTH)]
    sWSCR = [[R.new_sem(f"wscr{l}_{g}") for g in range(NGRP)] for l in range(DEPTH)]

    def PSf(k):
        return PS[:, k, :]

    def PSb(k):
        return PS[:, k, :].bitcast(BF16)

    IDF = CF[:, 0:128]
    INVF = CF[:, 128:129]
    SIGN = CF[:, 129:130]
    INVC = CF[:, 130:162].rearrange("p (c t) -> p c t", c=2)
    IDB = CB[:, 0:128]
    RM = CB[:, 128:256]
    ONES = CB[:, 256:384]
    MCUR = CB[:, 384:896]
    MPREV = CB[:, 896:1408]

    def vcol(l, r):
        c = l * VROWS_L + r
        return VEC[:, c:c + 1]

    def I(method, **kw):
        return lambda e: getattr(e, method)(**kw)

    def IS(items):
        items = list(items)

        def fn(e):
            first = ins = None
            for m_, kw in items:
                ins = getattr(e, m_)(**kw)
                if first is None:
                    first = ins
            return first, ins
        return fn

    def act(fn, reads, writes):
        return R.op("act", fn, reads=reads, writes=writes)

    def dve(fn, reads, writes):
        return R.op("dve", fn, reads=reads, writes=writes)

    def pool(fn, reads, writes):
        return R.op("pool", fn, reads=reads, writes=writes)

    def pe(fn, reads, writes):
        return R.op("pe", fn, reads=reads, writes=writes)

    def mm_group(out_ap, pairs, reads, writes):
        n = len(pairs)
        return pe(IS([("matmul", dict(out=out_ap, lhsT=lt, rhs=rh, start=(i == 0), stop=(i == n - 1)))
                      for i, (lt, rh) in enumerate(pairs)]), reads, writes)

    def tsl(s):
        return slice(s * 512, (s + 1) * 512)

    for l in range(DEPTH):
        pool(I("memset", ap=UPS[l][:], constant=0.0), [], bUPS[l])
        pool(I("memset", ap=PPS[l][:], constant=0.0), [], bPPS[l])
        pool(I("memset", ap=UST[l][:], constant=0.0), [], bUST[l])
        pool(I("memset", ap=VT[l][:], constant=1.0), [], bVT[l])
        pool(I("memset", ap=KT[l][:, 0:128], constant=0.0), [], [bKT[l][0]])
    pool(I("memset", ap=WPOOL[:], constant=0.0), [], [bWPOOL])

    R.dma("sp", I("dma_start", out=CF[:], in_=cf_d[:, :]), sCONST, writes=[bCONST])
    if 'sk' not in SKIP:
        R.dma("sp", I("dma_start", out=SK[:], in_=sinks_d[0:1, :].partition_broadcast(128)), sCONST, writes=[bSK])
    if 'vr' not in SKIP:
        R.dma("sp", I("dma_start", out=VR[:], in_=vecs_d.rearrange("(g r) c -> r g c", r=128)), sCONST, writes=[bVR])
    if 'cb' not in SKIP:
        R.dma("pool", I("dma_start", out=CB[:], in_=cb_d[:, :]), sCONSTP, writes=[bCB])
    for l in range(DEPTH if 'wpool' not in SKIP else 0):
        for g in range(4):
            po = (g % 2) * 64
            R.dma("pool", I("dma_start", out=WPOOL[po:po + 64, l, g // 2, po:po + 64], in_=w_pool_d[l, g]),
                  sCONSTP, writes=[bWPOOL])
    for b in (bCONST, bSK, bVR):
        b.last_w = (sCONST, sCONST.count)
    for b in (bCB, bWPOOL):
        b.last_w = (sCONSTP, sCONSTP.count)

    def prep_list(l):
        groups = [[] for _ in range(NGRP)]

        def tA(t):
            return wscr[l * NTILE + t].rearrange("p (k c) -> p k c", k=8)

        def src(w2d, c0, ncol):
            return w2d[:, c0:c0 + ncol].rearrange("(k p) c -> p k c", p=128)

        def cp(t, dst, s_):
            groups[GRP_OF_TILE[t]].append((dst, s_))

        win = w_in_d[l]
        cp(0, tA(0)[:, :, 0:256], src(win, 0, 256))
        cp(0, tA(0)[:, :, 256:384], src(win, 1024, 128))
        cp(0, tA(0)[:, :, 384:512], src(win, 1536, 128))
        cp(1, tA(1)[:, :, 0:128], src(win, 1280, 128))
        cp(1, tA(1)[:, :, 128:256], src(win, 1152, 128))
        cp(1, tA(1)[:, :, 256:384], src(win, 1664, 128))
        cp(1, tA(1)[:, :, 384:512], src(win, 1408, 128))
        for qc in range(4):
            cp(2, tA(2)[:, :, qc * 128:qc * 128 + 64], src(win, 256 + 64 * qc, 64))
            cp(2, tA(2)[:, :, qc * 128 + 64:qc * 128 + 128], src(win, 256 + 64 * (4 + qc), 64))
        cp(3, tA(3)[:, :, 0:256], src(win, 768, 256))
        for j in range(8):
            t = tA(4 + j)
            for i in range(3):
                cp(4 + j, t[:, :, i * 128:(i + 1) * 128], src(win, 1792 + i * 1024 + j * 128, 128))
            cp(4 + j, t[:, 0:2, 384:512], src(wa_d[l], j * 128, 128))
            cp(4 + j, t[:, 2:6, 384:512], src(wb_d[l], j * 128, 128))
            cp(4 + j, t[:, 6:8, 384:512], src(wc_d[l], j * 128, 128))
        for h in range(2):
            cp(12 + h, tA(12 + h), src(wo_d[l], h * 512, 512))
        for m in range(11):
            cp(14 + m, tA(14 + m)[:, :, 0:256], src(wup_d[l], 256 * m, 256))
            cp(14 + m, tA(14 + m)[:, :, 256:512], src(wup_d[l], DFF + 256 * m, 256))
        for j in range(8):
            dst = wscr[l * NTILE + 25 + j][:, 0:2816].rearrange("p (k c) -> p k c", k=22)
            cp(25 + j, dst, wdn_d[l][:, j * 128:(j + 1) * 128].rearrange("(k p) c -> p k c", p=128))
        return groups

    prep_pending = []

    def prep_queue(l):
        for g, items in enumerate(prep_list(l)):
            for i, (d_, s_) in enumerate(items):
                prep_pending.append((l, g, d_, s_, i == len(items) - 1))

    def prep_issue(n):
        for _ in range(min(n, len(prep_pending))):
            l, g, d_, s_, last = prep_pending.pop(0)
            R.dma("pool", I("dma_start", out=d_, in_=s_), sWSCR[l][g])
            if last:
                bWSCR[l][g].last_w = (sWSCR[l][g], sWSCR[l][g].count)

    def prep_flush_layer(l):
        while any(p[0] <= l for p in prep_pending):
            prep_issue(1)

    if 'prep' not in SKIP:
        prep_queue(0)
        prep_flush_layer(0)

    for g in range(6 if 'vec' not in SKIP else 0):
        k, hp = rPS.alloc()
        pe(I("transpose", out=PSf(k)[:, 0:128], in_=VR[:, g, :], identity=IDF), [bVR, bCONST], [hp])
        act(I("activation", out=VEC[:, g * 128:(g + 1) * 128], in_=PSf(k)[:, 0:128], func=AF.Copy), [hp], [bVEC])
    if 'sk' not in SKIP:
        act(I("activation", out=EXPS[:], in_=SK[:], func=AF.Exp), [bSK], [bEXPS])

    NSEQ = NCH * DEPTH * NTILE
    wstate = {"next": 0}

    def load_ahead(upto):
        upto = min(upto, NSEQ - 1)
        while wstate["next"] <= upto:
            i = wstate["next"]
            wstate["next"] += 1
            l = (i // NTILE) % DEPTH
            t = i % NTILE
            sl = i % nslot
            bWS[sl].gen = i
            ncol = 2816 if t >= 25 else 4096
            if t == 3:
                o_ = WS[sl][:].rearrange("p (k c) -> p k c", k=8)[:, :, 0:256]
                i_ = wscr[l * NTILE + t].rearrange("p (k c) -> p k c", k=8)[:, :, 0:256]
            else:
                o_ = WS[sl][:, 0:ncol]
                i_ = wscr[l * NTILE + t][:, 0:ncol]
            assert bWSCR[l][GRP_OF_TILE[t]].last_w is not None or 'prep' in SKIP, (l, t)
            R.dma("sp", I("dma_start", out=o_, in_=i_), sWS[sl], reads=[bWSCR[l][GRP_OF_TILE[t]]], writes=[bWS[sl]])

    def wtile(c, l, t, hold=0):
        i = (c * DEPTH + l) * NTILE + t
        if c == 0:
            prep_issue(3)
        load_ahead(i + nslot - 1 - hold)
        sl = i % nslot
        return sl, (bWS[sl], i)

    def WA(sl, n, k):
        return WS[sl][:, k * 512 + n * 128:k * 512 + n * 128 + 128]

    def proj(sl, hw, n, s):
        k, hp = rPS.alloc()
        mm_group(PSf(k), [(WA(sl, n, kk), XB[:, kk, tsl(s)]) for kk in range(8)],
                 [hw] + [bXB[kk][s] for kk in range(8)], [hp])
        return k, hp

    def proj4(sl, hw, s):
        banks = [rPS.alloc() for _ in range(4)]
        for kk in range(8):
            pe(IS([("matmul", dict(out=PSf(banks[n][0]), lhsT=WA(sl, n, kk), rhs=XB[:, kk, tsl(s)],
                                   start=(kk == 0), stop=(kk == 7))) for n in range(4)]),
               [hw, bXB[kk][s]], [b_[1] for b_ in banks])
        return banks

    def ln_pre(j, s):
        sl_ = tsl(s)
        act(I("activation", out=XB[:, j, sl_], in_=X[:, j, sl_], func=AF.Copy), [bX[j][s]], [bXB[j][s]])
        act(I("activation", out=LNS[:, j, :], in_=X[:, j, sl_], func=AF.Square), [bX[j][s]], [bLNS[j]])
        if j == 7:
            act(I("activation", out=DEN[:, 1, 4:5], in_=CF[:, 0:1], func=AF.Ln, bias=1.0, scale=1.0), [bCONST], [bDUM])

    def ln_stats(s, js, st):
        sl_ = tsl(s)
        first = st is None
        if first:
            st = (rPS.alloc(), rPS.alloc())
        (km, hm), (kq, hq) = st
        last = (js[-1] == 7)
        pe(IS([("matmul", dict(out=PSf(km), lhsT=ONES, rhs=XB[:, j, sl_], start=(first and j == js[0]),
                               stop=(last and j == 7))) for j in js]), [bCB] + [bXB[j][s] for j in js], [hm])
        pe(IS([("matmul", dict(out=PSf(kq), lhsT=ONES, rhs=LNS[:, j, :], start=(first and j == js[0]),
                               stop=(last and j == 7))) for j in js]), [bCB] + [bLNS[j] for j in js], [hq])
        return st

    def ln_post(l, s, grow, brow, st=None):
        sl_ = tsl(s)
        if st is None:
            st = ln_stats(s, list(range(8)), None)
        else:
            st = ln_stats(s, [7], st)
        (km, hm), (kq, hq) = st
        act(I("activation", out=MEAN[:], in_=PSf(km), func=AF.Copy), [hm], [bMEAN])
        k2, h2 = rFR.alloc()
        V2 = FR[k2][:, 0:512]
        dve(I("tensor_tensor", out=V2, in0=MEAN[:], in1=MEAN[:], op=ALU.mult), [bMEAN], [h2])
        dve(I("tensor_tensor", out=V2, in0=PSf(kq), in1=V2, op=ALU.subtract), [hq, h2], [h2])
        act(I("activation", out=V2, in_=V2, func=AF.Ln, bias=LN_EPS, scale=1.0), [h2], [h2])
        act(I("activation", out=RSTD[:], in_=V2, func=AF.Exp, scale=-0.5), [h2], [bRSTD])
        dve(I("scalar_tensor_tensor", out=MEAN[:], in0=MEAN[:], scalar=-1.0, in1=RSTD[:], op0=ALU.mult, op1=ALU.mult),
            [bMEAN, bRSTD], [bMEAN])
        for j in range(8):
            kx, hx = rFR.alloc()
            XN = FR[kx][:, 0:512]
            dve(I("tensor_tensor", out=XN, in0=X[:, j, sl_], in1=RSTD[:], op=ALU.mult), [bX[j][s], bRSTD], [hx])
            dve(I("tensor_tensor", out=XN, in0=XN, in1=MEAN[:], op=ALU.add), [hx, bMEAN], [hx])
            act(I("activation", out=XB[:, j, sl_], in_=XN, func=AF.Identity, scale=vcol(l, grow + j),
                  bias=vcol(l, brow + j)), [hx, bVEC], [bXB[j][s]])
            act(I("activation", out=X[:, j, sl_], in_=XN, func=AF.Identity, scale=vcol(l, grow + j),
                  bias=vcol(l, brow + j)), [hx, bVEC], [bX[j][s]])

    def prefetch_x(c):
        tok0 = c * T
        for tb in range(NB):
            R.dma("sp", I("dma_start", out=XIN[tb][:], in_=x_d[tok0 + tb * 128:tok0 + (tb + 1) * 128, :]), sXIN[tb],
                  writes=[bXIN[tb]])

    def tables(c):
        tok0 = c * T
        R.dma("sp", I("dma_start", out=POSI[:], in_=pos_d[0:1, tok0:tok0 + T].partition_broadcast(128)), sPOS,
              writes=[bPOSI])
        dve(I("tensor_copy", out=ANG[:], in_=POSI[:]), [bPOSI], [bANG])
        dve(I("tensor_scalar", out=ANG[:], in0=ANG[:], scalar1=INVF, scalar2=None, op0=ALU.mult), [bANG, bCONST], [bANG])
        C1 = 6.28125
        C2 = 2 * PI - 6.28125
        for (TB_, off) in ((ST, 0.0), (CT, 0.5 * PI)):
            dve(I("tensor_scalar", out=TB_[:], in0=ANG[:], scalar1=off, scalar2=None, op0=ALU.add), [bANG], [bTAB])
            dve(I("tensor_scalar", out=POSI[:], in0=TB_[:], scalar1=1.0 / (2 * PI), scalar2=None, op0=ALU.mult),
                [bTAB], [bPOSI])
            dve(I("tensor_copy", out=TMPT[:], in_=POSI[:]), [bPOSI], [bTMPT])
            dve(I("scalar_tensor_tensor", out=TB_[:], in0=TMPT[:], scalar=-C1, in1=TB_[:], op0=ALU.mult, op1=ALU.add),
                [bTMPT, bTAB], [bTAB])
            dve(I("scalar_tensor_tensor", out=TB_[:], in0=TMPT[:], scalar=-C2, in1=TB_[:], op0=ALU.mult, op1=ALU.add),
                [bTMPT, bTAB], [bTAB])
            dve(I("tensor_scalar", out=TB_[:], in0=TB_[:], scalar1=-PI, scalar2=PI, op0=ALU.max, op1=ALU.min),
                [bTAB], [bTAB])
            act(I("activation", out=TB_[:], in_=TB_[:], func=AF.Sin), [bTAB], [bTAB])
        dve(I("tensor_scalar", out=ST[:], in0=ST[:], scalar1=SIGN, scalar2=None, op0=ALU.mult), [bTAB, bCONST], [bTAB])

    def prologue(c):
        for tb in range(NB if 'xin' not in SKIP else 0):
            s = tb // 4
            xs = slice(tb * 128, (tb + 1) * 128)
            for half in range(2):
                k, hp = rPS.alloc()
                pe(IS([("transpose", dict(out=PSf(k)[:, q * 128:(q + 1) * 128],
                                          in_=XIN[tb][:, (half * 4 + q) * 128:(half * 4 + q + 1) * 128], identity=IDF))
                       for q in range(4)]), [bXIN[tb], bCONST], [hp])
                p3 = PSf(k).rearrange("p (q t) -> p q t", q=4)
                act(I("activation", out=X[:, half * 4:half * 4 + 4, xs], in_=p3, func=AF.Copy), [hp],
                    [bX[half * 4 + q][s] for q in range(4)])
                dve(I("tensor_copy", out=XB[:, half * 4:half * 4 + 4, xs], in_=p3), [hp],
                    [bXB[half * 4 + q][s] for q in range(4)])

    def layer(c, l):
        sl0, hw0 = wtile(c, l, 0)
        t0banks = [proj4(sl0, hw0, s) for s in range(NSUB)]
        for s in range(NSUB):
            hpl = bPL[s]
            PL = PLB[s]
            first = (c == 0 and s == 0)
            for cc in range(2):
                kp, hp = t0banks[s][cc]
                ku, hu = rFR.alloc()
                U = FR[ku]
                act(I("activation", out=U[:, 1:16], in_=UPS[l][:, cc, :], func=AF.Copy), [bUPS[l][cc]], [hu])
                act(I("activation", out=U[:, 16:528], in_=PSf(kp), func=AF.Copy), [hp], [hu])
                act(I("activation", out=UPS[l][:, cc, :], in_=U[:, 513:528], func=AF.Copy), [hu], [bUPS[l][cc]])
                k1, h1 = rFR.alloc()
                A1 = FR[k1]
                pool(I("tensor_tensor", out=A1[:, 2:528], in0=U[:, 2:528], in1=U[:, 1:527], op=ALU.add), [hu], [h1])
                k2, h2 = rFR.alloc()
                A2 = FR[k2]
                pool(I("tensor_tensor", out=A2[:, 4:528], in0=A1[:, 4:528], in1=A1[:, 2:526], op=ALU.add), [h1], [h2])
                if cc == 0:
                    grp = [(0, A1, h1, 0.5), (64, A2, h2, 0.25)]
                else:
                    k3, h3 = rFR.alloc()
                    A3 = FR[k3]
                    pool(I("tensor_tensor", out=A3[:, 8:528], in0=A2[:, 8:528], in1=A2[:, 4:524], op=ALU.add),
                         [h2], [h3])
                    k4, h4 = rFR.alloc()
                    A4 = FR[k4]
                    pool(I("tensor_tensor", out=A4[:, 16:528], in0=A3[:, 16:528], in1=A3[:, 8:520], op=ALU.add),
                         [h3], [h4])
                    grp = [(0, A3, h3, 0.125), (64, A4, h4, 0.0625)]
                for (p0, A, hA, inv) in grp:
                    ps_ = slice(p0, p0 + 64)
                    dve(I("scalar_tensor_tensor", out=PL[ps_, cc, :], in0=A[ps_, 16:528], scalar=inv,
                          in1=U[ps_, 16:528], op0=ALU.mult, op1=ALU.subtract), [hA, hu], [hpl])
                    if first:
                        kf, hf = rFR.alloc()
                        pool(I("tensor_tensor", out=FR[kf][ps_, 0:15], in0=A[ps_, 16:31], in1=INVC[ps_, cc, 0:15],
                               op=ALU.mult), [hA, bCONST], [hf])
                        pool(I("tensor_tensor", out=PL[ps_, cc, 0:15], in0=FR[kf][ps_, 0:15], in1=U[ps_, 16:31],
                               op=ALU.subtract), [hf, hu], [hpl])

        if DBG <= 1:
            return
        sl1, hw1 = wtile(c, l, 1, hold=1)
        for s in range(NSUB):
            for h in range(2):
                (sx, hx_, nx) = (sl0, hw0, 2) if h == 0 else (sl1, hw1, 1)
                (sg, hg_, ng) = (sl0, hw0, 3) if h == 0 else (sl1, hw1, 2)
                nb_ = 0 if h == 0 else 3
                if h == 0:
                    (kxc, hxc), (kgc, hgc) = t0banks[s][2], t0banks[s][3]
                else:
                    kxc, hxc = proj(sx, hx_, nx, s)
                    kgc, hgc = proj(sg, hg_, ng, s)
                kgb, hgb = proj(sl1, hw1, nb_, s)
                kx, hx = rFR.alloc()
                XC = FR[kx][:, 0:512]
                act(I("activation", out=XC, in_=PSf(kxc), func=AF.Copy), [hxc], [hx])
                kpp, hpp = rFR.alloc()
                PP = FR[kpp]
                dve(I("tensor_copy", out=PP[:, 0:2], in_=PPS[l][:, h, :]), [bPPS[l][h]], [hpp])
                dve(I("tensor_tensor", out=PP[:, 2:514], in0=PSf(kgc), in1=XC, op=ALU.mult), [hgc, hx], [hpp])
                dve(I("tensor_copy", out=PPS[l][:, h, :], in_=PP[:, 512:514]), [hpp], [bPPS[l][h]])
                ky, hy = rFR.alloc()
                Y = FR[ky][:, 0:512]
                act(I("activation", out=Y, in_=PP[:, 0:512], func=AF.Identity, scale=vcol(l, 164 + h)),
                    [hpp, bVEC], [hy])
                dve(I("scalar_tensor_tensor", out=Y, in0=PP[:, 1:513], scalar=vcol(l, 166 + h), in1=Y,
                      op0=ALU.mult, op1=ALU.add), [hpp, hy, bVEC], [hy])
                dve(I("scalar_tensor_tensor", out=Y, in0=PP[:, 2:514], scalar=vcol(l, 168 + h), in1=Y,
                      op0=ALU.mult, op1=ALU.add), [hpp, hy, bVEC], [hy])
                dve(I("tensor_tensor", out=OCv(h, tsl(s)), in0=PSf(kgb), in1=Y, op=ALU.mult), [hgb, hy], [bOC[h][s]])

        if DBG <= 2:
            return
        sl2, hw2 = wtile(c, l, 2)
        sl3, hw3 = wtile(c, l, 3, hold=1)

        def rope_rot(qc, s, QB, hb_, T1, h1):
            kr, hr = rPS.alloc()
            mm_group(PSf(kr), [(RM, QB)], [bCB, hb_], [hr])
            k2, h2 = rFR.alloc()
            T2 = FR[k2][:, 0:512]
            dve(I("tensor_tensor", out=T2, in0=PSf(kr), in1=ST[:, tsl(s)], op=ALU.mult), [hr, bTAB], [h2])
            if qc < 4:
                dst = QTv(qc, tsl(s))
                wr = [bQT[qc][s]]
            else:
                dst = KT[l][:, 128 + s * 512:128 + (s + 1) * 512]
                wr = [bKT[l][s + 1]]
            pool(I("tensor_tensor", out=dst, in0=T1, in1=T2, op=ALU.add), [h1, h2], wr)

        for s in range(NSUB):
            pend = None
            for qc in (4, 0, 1, 2, 3):
                if qc < 4:
                    kq, hq = proj(sl2, hw2, qc, s)
                else:
                    kq, hq = proj(sl3, hw3, 0, s)
                kb_, hb_ = rBR.alloc()
                QB = BR[kb_][:, 0:512]
                act(I("activation", out=QB, in_=PSf(kq), func=AF.Copy), [hq], [hb_])
                k1, h1 = rFR.alloc()
                T1 = FR[k1][:, 0:512]
                dve(I("tensor_tensor", out=T1, in0=PSf(kq), in1=CT[:, tsl(s)], op=ALU.mult), [hq, bTAB], [h1])
                if pend is not None:
                    rope_rot(*pend)
                pend = (qc, s, QB, hb_, T1, h1)
            kv, hv = rPS.alloc()
            items = []
            for blk in range(4):
                ts_ = slice(s * 512 + blk * 128, s * 512 + (blk + 1) * 128)
                for kk in range(8):
                    items.append(("matmul", dict(out=PSf(kv)[:, blk * 128:(blk + 1) * 128], lhsT=XB[:, kk, ts_],
                                                 rhs=WA(sl3, 1, kk), start=(kk == 0), stop=(kk == 7))))
            pe(IS(items), [hw3] + [bXB[kk][s] for kk in range(8)], [hv])
            rope_rot(*pend)
            for blk in range(4):
                act(I("activation", out=VT[l][:, 1 + s * 4 + blk, :, 0:64],
                      in_=PSf(kv)[:, blk * 128:(blk + 1) * 128].rearrange("p (g d) -> p g d", g=2), func=AF.Copy),
                    [hv], [bVT[l][s + 1]])
        if DBG <= 3:
            return

        def attn_scores(n):
            s = n // 4
            kbs = [n - 1, n]
            if c == 0 and n == 0:
                kbs = [n]
            plist = []
            for g in range(2):
                gp = slice(g * 64, (g + 1) * 64)
                for kb in kbs:
                    sk = 0 if kb < 0 else 1 + kb // 4
                    kst, hst = rPS.alloc()
                    M = MCUR if kb == n else MPREV
                    pe(IS([("matmul", dict(out=PSf(kst), lhsT=KT[l][gp, (kb + 1) * 128:(kb + 2) * 128],
                                           rhs=HR[gp, 16:20, n * 128:(n + 1) * 128], start=True, stop=False)),
                           ("matmul", dict(out=PSf(kst), lhsT=IDB, rhs=M, start=False, stop=True))]),
                       [bKT[l][sk], bCB] + [bQT[q][s] for q in range(4)], [hst])
                    kp_, hp_ = rPB.alloc()
                    P = PB[kp_][:]
                    act(I("activation", out=P, in_=PSf(kst), func=AF.Exp, scale=0.125), [hst], [hp_])
                    plist.append((g, kb, sk, P, hp_))
            return plist, kbs

        def attn_pv(n, plist, kbs):
            s = n // 4
            ko = [rPS.alloc(), rPS.alloc()]
            for (g, kb, sk, P, hp_) in plist:
                pe(IS([("matmul", dict(out=PSf(ko[g][0])[:, hh * 65:(hh + 1) * 65],
                                       lhsT=P[:, hh * 128:(hh + 1) * 128], rhs=VT[l][:, kb + 1, g, :],
                                       start=(kb == kbs[0] and hh == 0), stop=(kb == kbs[-1] and hh == 3),
                                       skip_group_check=True)) for hh in range(4)]),
                   [hp_, bVT[l][sk]], [ko[g][1]])
            kob, hob = rBR.alloc()
            OBT = BR[kob][:, 0:512]
            for g in range(2):
                O3 = PSf(ko[g][0])[:, 0:260].rearrange("p (h d) -> p h d", h=4)
                dve(I("tensor_tensor", out=DEN[:, g, 0:4], in0=O3[:, :, 64],
                      in1=EXPS[:, l * 8 + g * 4:l * 8 + g * 4 + 4], op=ALU.add), [ko[g][1], bEXPS], [bDEN])
                dve(I("reciprocal", out=RDEN[:, g, 0:4], in_=DEN[:, g, 0:4]), [bDEN], [bRDEN])
                dve(I("tensor_tensor", out=OBT[:, g * 256:(g + 1) * 256].rearrange("p (h d) -> p h d", h=4),
                      in0=O3[:, :, 0:64], in1=RDEN[:, g, 0:4].unsqueeze(2).broadcast_to([128, 4, 64]), op=ALU.mult),
                    [ko[g][1], bRDEN], [hob])
            ktp, htp = rPS.alloc()
            pe(IS([("transpose", dict(out=PSb(ktp)[:, cq * 128:(cq + 1) * 128], in_=OBT[:, cq * 128:(cq + 1) * 128],
                                      identity=IDB)) for cq in range(4)]), [hob, bCB], [htp])
            dve(I("tensor_copy", out=HR[:, 8:12, n * 128:(n + 1) * 128],
                  in_=PSb(ktp)[:, 0:512].rearrange("p (q t) -> p q t", q=4)),
                [htp], [bOB[q][s] for q in range(4)])

        pend = None
        for n in range(NB):
            cur = attn_scores(n)
            if pend is not None:
                attn_pv(*pend)
            pend = (n,) + cur
        attn_pv(*pend)
        if l == DEPTH - 1 and c + 1 < NCH and 'tables' not in SKIP:
            tables(c + 1)
        pool(I("tensor_copy", out=KT[l][:, 0:128], in_=KT[l][:, NB * 128:(NB + 1) * 128]), [bKT[l][NSUB]], [bKT[l][0]])
        pool(I("tensor_copy", out=VT[l][:, 0, :, :], in_=VT[l][:, NB, :, :]), [bVT[l][NSUB]], [bVT[l][0]])

        if DBG <= 4:
            return
        for s in range(NSUB):
            for cc in range(2):
                km, hm = rPS.alloc()
                mm_group(PSf(km), [(WPOOL[:, l, cc, :], PLB[s][:, cc, :])], [bWPOOL, bPL[s]], [hm])
                act(I("activation", out=OAv(cc, tsl(s)), in_=PSf(km), func=AF.Identity, scale=vcol(l, 170 + cc)),
                    [hm, bVEC], [bOA[cc][s]])
        for j in range(8):
            slj, hwj = wtile(c, l, 4 + j)
            for s in range(NSUB):
                srcs = [(0, 2, [OAv(kk, tsl(s)) for kk in range(2)], [bOA[0][s], bOA[1][s]]),
                        (2, 6, [OBv(kk, tsl(s)) for kk in range(4)], [bOB[q][s] for q in range(4)]),
                        (6, 8, [OCv(kk, tsl(s)) for kk in range(2)], [bOC[0][s], bOC[1][s]])]
                ms = []
                for i in range(3):
                    kg, hg = proj(slj, hwj, i, s)
                    k0, k1_, aps, rb = srcs[i]
                    kbp, hbp = rPS.alloc()
                    mm_group(PSf(kbp), [(WA(slj, 3, kk), aps[kk - k0]) for kk in range(k0, k1_)], [hwj] + rb, [hbp])
                    ksg, hsg = rFR.alloc()
                    SG = FR[ksg][:, 0:512]
                    act(I("activation", out=SG, in_=PSf(kg), func=AF.Sigmoid), [hg], [hsg])
                    dve(I("tensor_tensor", out=SG, in0=PSf(kbp), in1=SG, op=ALU.mult), [hbp, hsg], [hsg])
                    ms.append((SG, hsg))
                pool(I("tensor_tensor", out=ms[0][0], in0=ms[0][0], in1=ms[1][0], op=ALU.add),
                     [ms[0][1], ms[1][1]], [ms[0][1]])
                pool(I("tensor_tensor", out=MGv(j, tsl(s)), in0=ms[0][0], in1=ms[2][0], op=ALU.add),
                     [ms[0][1], ms[2][1]], [bMG[j][s]])

        if DBG <= 5:
            return
        slo = [wtile(c, l, 12), wtile(c, l, 13, hold=1)]
        for s in range(NSUB):
            for j in range(8):
                so, ho = slo[j // 4]
                km, hm = rPS.alloc()
                mm_group(PSf(km), [(WA(so, j % 4, kk), MGv(kk, tsl(s))) for kk in range(8)],
                         [ho] + [bMG[kk][s] for kk in range(8)], [hm])
                if j == 7:
                    st = ln_stats(s, list(range(7)), None)
                dve(I("scalar_tensor_tensor", out=X[:, j, tsl(s)], in0=X[:, j, tsl(s)], scalar=ALPHA, in1=PSf(km),
                      op0=ALU.mult, op1=ALU.add), [hm, bX[j][s]], [bX[j][s]])
                ln_pre(j, s)
            ln_post(l, s, 132, 140, st)

        if DBG <= 6:
            return
        for m in range(11):
            slm, hwm = wtile(c, l, 14 + m)
            for s in range(NSUB):
                mb = proj4(slm, hwm, s) if (m == 0 and NSUB == 1) else None
                for i in range(2):
                    fa = 2 * m + i
                    fb = 22 + fa
                    ys = []
                    for (f, n_) in ((fa, i), (fb, 2 + i)):
                        kp, hp = mb[n_] if mb is not None else proj(slm, hwm, n_, s)
                        P = PSf(kp)
                        HL = UST[l][:, f, :]
                        ky, hy = rFR.alloc()
                        Y = FR[ky][:, 0:512]
                        w0, w1, w2 = vcol(l, f), vcol(l, 44 + f), vcol(l, 88 + f)
                        act(I("activation", out=Y[:, 2:512], in_=P[:, 0:510], func=AF.Identity, scale=w0),
                            [hp, bVEC], [hy])
                        act(I("activation", out=Y[:, 0:2], in_=HL, func=AF.Identity, scale=w0),
                            [bUST[l][f], bVEC], [hy])
                        dve(I("scalar_tensor_tensor", out=Y[:, 0:1], in0=HL[:, 1:2], scalar=w1, in1=Y[:, 0:1],
                              op0=ALU.mult, op1=ALU.add), [bUST[l][f], hy, bVEC], [hy])
                        dve(I("scalar_tensor_tensor", out=Y[:, 1:512], in0=P[:, 0:511], scalar=w1, in1=Y[:, 1:512],
                              op0=ALU.mult, op1=ALU.add), [hp, hy, bVEC], [hy])
                        dve(I("scalar_tensor_tensor", out=Y, in0=P, scalar=w2, in1=Y,
                              op0=ALU.mult, op1=ALU.add), [hp, hy, bVEC], [hy])
                        dve(I("tensor_copy", out=HL, in_=P[:, 510:512]), [hp], [bUST[l][f]])
                        ys.append((Y, hy))
                    (YA, hya), (YB, hyb) = ys
                    act(I("activation", out=YA, in_=YA, func=AF.Silu), [hya], [hya])
                    pool(I("tensor_tensor", out=HR[:, fa, tsl(s)], in0=YA, in1=YB, op=ALU.mult), [hya, hyb], [bH[fa][s]])

        if DBG <= 7:
            return
        for j in range(8):
            sld, hwd = wtile(c, l, 25 + j)
            for s in range(NSUB):
                kd, hd = rPS.alloc()
                if j == 0:
                    for kk in range(22):
                        pe(I("matmul", out=PSf(kd), lhsT=WS[sld][:, kk * 128:(kk + 1) * 128], rhs=HR[:, kk, tsl(s)],
                             start=(kk == 0), stop=(kk == 21)), [hwd, bH[kk][s]], [hd])
                else:
                    mm_group(PSf(kd), [(WS[sld][:, kk * 128:(kk + 1) * 128], HR[:, kk, tsl(s)]) for kk in range(22)],
                             [hwd] + [bH[kk][s] for kk in range(22)], [hd])
                st2 = None
                if NSUB == 1 and j == 7:
                    st2 = ln_stats(s, list(range(7)), None)
                dve(I("scalar_tensor_tensor", out=X[:, j, tsl(s)], in0=X[:, j, tsl(s)], scalar=ALPHA, in1=PSf(kd),
                      op0=ALU.mult, op1=ALU.add), [hd, bX[j][s]], [bX[j][s]])
                if NSUB == 1:
                    ln_pre(j, s)
        for s in range(NSUB):
            if NSUB > 1:
                for j in range(8):
                    ln_pre(j, s)
            ln_post(l, s, 148, 156, st2 if NSUB == 1 else None)

    def epilogue(c):
        tok0 = c * T
        for tb in range(NB):
            s = tb // 4
            ki, hi = rIO.alloc()
            for half in range(2 if 'epi' not in SKIP else 0):
                k, hp = rPS.alloc()
                pe(IS([("transpose", dict(out=PSf(k)[:, q * 128:(q + 1) * 128],
                                          in_=X[:, half * 4 + q, tb * 128:(tb + 1) * 128], identity=IDF))
                       for q in range(4)]), [bX[half * 4 + q][s] for q in range(4)] + [bCONST], [hp])
                if half == 0:
                    act(I("activation", out=IO[ki][:, 0:512], in_=PSf(k), func=AF.Copy), [hp], [hi])
                else:
                    dve(I("tensor_copy", out=IO[ki][:, 512:1024], in_=PSf(k)), [hp], [hi])
            R.dma("sp", I("dma_start", out=out_d[tok0 + tb * 128:tok0 + (tb + 1) * 128, :], in_=IO[ki][:]), sIO[ki],
                  reads=[hi])

    prefetch_x(0)
    if 'tables' not in SKIP:
        tables(0)
    for c in range(NCH):
        prologue(c)
        for l in range(DEPTH):
            if l == DEPTH - 1 and c + 1 < NCH:
                prefetch_x(c + 1)
            if c == 0 and l + 1 < DEPTH and 'prep' not in SKIP:
                prep_queue(l + 1)
            if DBG > 0:
                layer(c, l)
            if c == 0 and l + 1 < DEPTH and 'prep' not in SKIP:
                prep_flush_layer(l + 1)
        epilogue(c)

    for i in range(2):
        R.wait_only("sp", (sIO[i], sIO[i].count))

    for s_ in R.sems:
        s_.h = nc.alloc_semaphore(s_.name)

    def run(eng, lst):
        for waits, fn, sem, step in lst:
            if fn is None:
                for (ws, wv) in waits:
                    eng.wait_ge(ws.h, wv)
                continue
            for (ws, wv) in waits[:-1]:
                eng.wait_ge(ws.h, wv)
            r = fn(eng)
            first, last = r if isinstance(r, tuple) else (r, r)
            if waits:
                first._wait_ge(waits[-1][0].h, waits[-1][1])
            last.then_inc(sem.h, step)

    with nc.Block() as block:
        @block.tensor
        def _(e):
            run(e, R.ops["pe"])

        @block.scalar
        def _(e):
            run(e, R.ops["act"])

        @block.vector
        def _(e):
            run(e, R.ops["dve"])

        @block.gpsimd
        def _(e):
            run(e, R.ops["pool"])

        @block.sync
        def _(e):
            run(e, R.ops["sp"])
    stats = {e: len(v) for e, v in R.ops.items()}
    stats["sem_counts"] = {s_.name: s_.count for s_ in R.sems}
    return nc, stats


def make_consts():
    cf = np.zeros((128, 192), np.float32)
    cf[:, 0:128] = np.eye(128, dtype=np.float32)
    inv_freq = (np.float32(500000.0) ** (-np.arange(0, 16, 2, dtype=np.float32) / np.float32(16))).astype(np.float32)
    for p in range(128):
        r = p % 64
        if r < 16:
            cf[p, 128] = inv_freq[r % 8]
            cf[p, 129] = -1.0 if r < 8 else 1.0
        for cc in range(2):
            w = (2, 4, 8, 16)[cc * 2 + (p // 64)]
            for t in range(16):
                cf[p, 130 + cc * 16 + t] = 1.0 / min(t + 1, w)
    cb = np.zeros((128, 1408), np.float32)
    cb[:, 0:128] = np.eye(128, dtype=np.float32)
    for m in range(128):
        r = m % 64
        if r < 8:
            cb[m + 8, 128 + m] = 1.0
        elif r < 16:
            cb[m - 8, 128 + m] = 1.0
    cb[:, 256:384] = 1.0 / 1024.0
    jj = np.arange(128)[:, None]
    ii = np.arange(128)[None, :]
    cur = np.where(jj <= ii, 0.0, -30000.0).astype(np.float32)
    prev = np.where(jj > ii, 0.0, -30000.0).astype(np.float32)
    cb[:, 384:896] = np.tile(cur, (1, 4))
    cb[:, 896:1408] = np.tile(prev, (1, 4))
    return cf, cb


def make_vecs(depth, ffn_conv_w, ln1_g, ln1_b, ln2_g, ln2_b, conv_w, pool_scale):
    rows = []
    for l in range(depth):
        rows.append(np.asarray(ffn_conv_w[l], np.float32).reshape(132, 128))
        rows.append(np.asarray(ln1_g[l], np.float32).reshape(8, 128))
        rows.append(np.asarray(ln1_b[l], np.float32).reshape(8, 128))
        rows.append(np.asarray(ln2_g[l], np.float32).reshape(8, 128))
        rows.append(np.asarray(ln2_b[l], np.float32).reshape(8, 128))
        rows.append(np.asarray(conv_w[l], np.float32).reshape(6, 128))
        rows.append(np.asarray(pool_scale[l], np.float32).reshape(2, 128))
    v = np.concatenate(rows, axis=0)
    out = np.zeros((768, 128), np.float32)
    out[:v.shape[0]] = v
    return out


_CACHE = {}


def run_model(inputs, S, DEPTH, T, n_cores, trace=False):
    key = (S, DEPTH, T)
    if key not in _CACHE:
        _CACHE[key] = build(S, DEPTH, T)
    nc, stats = _CACHE[key]
    cf, cb = make_consts()
    f = lambda k: np.ascontiguousarray(np.asarray(inputs[k], np.float32)[:DEPTH])
    vecs = make_vecs(DEPTH, inputs["ffn_conv_w"], inputs["ln1_g"], inputs["ln1_b"], inputs["ln2_g"], inputs["ln2_b"],
                     inputs["conv_w"], inputs["pool_scale"])
    sinks = np.zeros((1, 32), np.float32)
    sinks[0, :DEPTH * 8] = np.asarray(inputs["attn_sinks"], np.float32)[:DEPTH].reshape(-1)
    shared = {
        "w_in": f("w_in"), "w_pool": f("w_pool"), "w_branch_a": f("w_branch_a"), "w_branch_b": f("w_branch_b"),
        "w_branch_c": f("w_branch_c"), "w_o": f("w_o"), "w_up": f("w_up"), "w_down": f("w_down"),
        "vecs": vecs, "sinks": sinks, "cf32": cf, "cbf": cb,
    }
    x = np.asarray(inputs["x"], np.float32)
    pos = np.asarray(inputs["positions"], np.int32)
    in_maps = []
    for b in range(n_cores):
        m = dict(shared)
        m["x"] = np.ascontiguousarray(x[b, :S])
        m["pos"] = np.ascontiguousarray(pos[b, :S]).reshape(1, S)
        in_maps.append(m)
    res = run_bass_kernel_spmd(nc, in_maps, core_ids=list(range(n_cores)), trace=trace)
    out = np.stack([np.asarray(r["out"], np.float32) for r in res.results], axis=0)
    return out, res


def kernel(**inputs):
    out, _ = run_model(inputs, 8192, 4, 512, 8)
    return out
```
